# Optimizing a Trainium2 kernel written in Bass

```python
import jax, jax.numpy as jnp
from jax import lax
import numpy as np

D_MODEL = 2048
BATCH = 1
SEQ = 8192
DEPTH = 1

PLE_DIM = 256
EPS = 1e-6

SSM_HEAD_DIM = 64
SSM_INNER = D_MODEL
SSM_HEADS = SSM_INNER // SSM_HEAD_DIM
SSM_GROUPS = 4
SSM_STATE = 128
SSM_CONV = 4
SSM_CHUNK = 128
SSM_HPG = SSM_HEADS // SSM_GROUPS
SSM_CONV_CH = SSM_INNER + 2 * SSM_GROUPS * SSM_STATE

ATT_HEAD_DIM = 64
ATT_HEADS = D_MODEL // ATT_HEAD_DIM
ATT_KV_HEADS = ATT_HEADS // 4
ATT_REP = ATT_HEADS // ATT_KV_HEADS
ATT_INNER = ATT_HEADS * ATT_HEAD_DIM
ATT_KV_DIM = ATT_KV_HEADS * ATT_HEAD_DIM
WINDOW = 128
ROPE_THETA = 500000.0
ROPE_DIM = ATT_HEAD_DIM // 4

MIX_WIDTH = SSM_INNER + ATT_INNER
IN_WIDTH = SSM_INNER + SSM_CONV_CH + SSM_HEADS + ATT_INNER + 2 * ATT_KV_DIM

D_FF = 256 * ((8 * D_MODEL // 3 + 255) // 256)
FFN_CONV = 3

kernel_name = 'hymba_ssd_swa_convffn_ple'


def _rmsnorm(x, g):
    xf = x.astype(jnp.float32)
    y = xf * lax.rsqrt(jnp.mean(xf * xf, axis=-1, keepdims=True) + EPS)
    return (y * g.astype(jnp.float32)).astype(x.dtype)


def _causal_dwconv(x, w, b):
    k = w.shape[0]
    y = lax.conv_general_dilated(x, w[:, None, :].astype(x.dtype), window_strides=(1,),
                                 padding=[(k - 1, 0)],
                                 dimension_numbers=('NWC', 'WIO', 'NWC'),
                                 feature_group_count=x.shape[-1])
    return y + b.astype(x.dtype)


def _ssd(xs, dt, a, bm, cm):
    b, s = xs.shape[0], xs.shape[1]
    nc, L = s // SSM_CHUNK, SSM_CHUNK
    G, E, P, N = SSM_GROUPS, SSM_HPG, SSM_HEAD_DIM, SSM_STATE
    X = (xs.astype(jnp.float32) * dt[..., None]).reshape(b, nc, L, G, E, P)
    a_dt = (dt * a).reshape(b, nc, L, G, E).transpose(0, 3, 4, 1, 2)
    Bc = bm.astype(jnp.float32).reshape(b, nc, L, G, N)
    Cc = cm.astype(jnp.float32).reshape(b, nc, L, G, N)
    a_cs = jnp.cumsum(a_dt, axis=-1)
    tril = jnp.tril(jnp.ones((L, L), dtype=bool))
    seg = a_cs[..., :, None] - a_cs[..., None, :]
    decay_in = jnp.exp(jnp.where(tril, seg, -jnp.inf))
    cb = jnp.einsum('bclgn,bcsgn->bgcls', Cc, Bc)
    y_diag = jnp.einsum('bgecls,bcsgep->bclgep', cb[:, :, None] * decay_in, X)
    decay_states = jnp.exp(a_cs[..., -1:] - a_cs)
    states = jnp.einsum('bclgn,bgecl,bclgep->bcgepn', Bc, decay_states, X)
    chunk_decay = jnp.exp(a_cs[..., -1])

    def step(h, inp):
        st, d = inp
        return h * d[..., None, None] + st, h

    h0 = jnp.zeros((b, G, E, P, N), jnp.float32)
    _, prev = lax.scan(step, h0, (jnp.moveaxis(states, 1, 0), jnp.moveaxis(chunk_decay, -1, 0)))
    prev = jnp.moveaxis(prev, 0, 1)
    y_off = jnp.einsum('bclgn,bcgepn,bgecl->bclgep', Cc, prev, jnp.exp(a_cs))
    return (y_diag + y_off).reshape(b, s, SSM_HEADS, P)


def _partial_rope(t, cos, sin):
    half = ROPE_DIM // 2
    t1, t2 = t[..., :half], t[..., half:ROPE_DIM]
    return jnp.concatenate([t1 * cos - t2 * sin, t2 * cos + t1 * sin, t[..., ROPE_DIM:]], axis=-1)


def _swa_sinks(q, k, v, sinks):
    b, s = q.shape[0], q.shape[1]
    blk = WINDOW
    nb = s // blk
    scale = ATT_HEAD_DIM ** -0.5
    qb = q.astype(jnp.float32).reshape(b, nb, blk, ATT_KV_HEADS, ATT_REP, ATT_HEAD_DIM)

    def band(t):
        tp = jnp.pad(t.astype(jnp.float32), ((0, 0), (blk, 0), (0, 0), (0, 0)))
        tp = tp.reshape(b, nb + 1, blk, ATT_KV_HEADS, ATT_HEAD_DIM)
        return jnp.concatenate([tp[:, :-1], tp[:, 1:]], axis=2)

    kb, vb = band(k), band(v)
    sc = jnp.einsum('bnqhrd,bnkhd->bnhrqk', qb, kb) * scale
    qi = jnp.arange(blk)[:, None]
    kj = jnp.arange(2 * blk)[None, :]
    diff = qi + blk - kj
    in_band = (diff >= 0) & (diff < WINDOW)
    valid = (jnp.arange(nb)[:, None] * blk + kj - blk) >= 0
    mask = in_band[None] & valid[:, None, :]
    sc = jnp.where(mask[None, :, None, None], sc, -jnp.inf)
    sink = sinks.astype(jnp.float32).reshape(ATT_KV_HEADS, ATT_REP)[None, None, :, :, None, None]
    m = jnp.maximum(jnp.max(sc, axis=-1, keepdims=True), sink)
    e = jnp.exp(sc - m)
    pr = e / (jnp.sum(e, axis=-1, keepdims=True) + jnp.exp(sink - m))
    o = jnp.einsum('bnhrqk,bnkhd->bnqhrd', pr, vb)
    return o.reshape(b, s, ATT_INNER).astype(q.dtype)


def setup_inputs(seed: int = 0) -> dict:
    key = jax.random.key(seed)
    ks = jax.random.split(key, 24)
    f32 = jnp.float32
    nrm = lambda k, shape, sc: jax.random.normal(k, shape, f32) * sc
    gain = lambda k, shape: 1.0 + 0.02 * jax.random.normal(k, shape, f32)
    dt0 = jnp.exp(jax.random.uniform(ks[5], (DEPTH, SSM_HEADS), f32) * (np.log(0.1) - np.log(0.001)) + np.log(0.001))
    return {
        'x': nrm(ks[0], (BATCH, SEQ, D_MODEL), 1.0),
        'p': nrm(ks[1], (DEPTH, BATCH, SEQ, PLE_DIM), 1.0),
        'positions': jnp.broadcast_to(jnp.arange(SEQ, dtype=jnp.int32)[None, :], (BATCH, SEQ)),
        'attn_norm_g': gain(ks[2], (DEPTH, D_MODEL)),
        'w_in': nrm(ks[3], (DEPTH, D_MODEL, IN_WIDTH), D_MODEL ** -0.5),
        'conv_w': nrm(ks[4], (DEPTH, SSM_CONV, SSM_CONV_CH), SSM_CONV ** -0.5),
        'conv_b': nrm(ks[6], (DEPTH, SSM_CONV_CH), 0.02),
        'dt_bias': dt0 + jnp.log(-jnp.expm1(-dt0)),
        'a_log': jnp.log(jax.random.uniform(ks[7], (DEPTH, SSM_HEADS), f32, 1.0, 16.0)),
        'd_skip': gain(ks[8], (DEPTH, SSM_HEADS)),
        'ssm_norm_g': gain(ks[9], (DEPTH, SSM_INNER)),
        'q_norm_g': gain(ks[10], (DEPTH, ATT_HEAD_DIM)),
        'k_norm_g': gain(ks[11], (DEPTH, ATT_HEAD_DIM)),
        'sinks': nrm(ks[12], (DEPTH, ATT_HEADS), 0.5),
        'w_out': nrm(ks[13], (DEPTH, MIX_WIDTH, D_MODEL), MIX_WIDTH ** -0.5),
        'ffn_norm_g': gain(ks[14], (DEPTH, D_MODEL)),
        'w_up': nrm(ks[15], (DEPTH, D_MODEL, 2 * D_FF), D_MODEL ** -0.5),
        'ffn_conv_w': nrm(ks[16], (DEPTH, FFN_CONV, 2 * D_FF), FFN_CONV ** -0.5),
        'ffn_conv_b': nrm(ks[17], (DEPTH, 2 * D_FF), 0.02),
        'w_down': nrm(ks[18], (DEPTH, D_FF, D_MODEL), D_FF ** -0.5),
        'ple_norm_g': gain(ks[19], (DEPTH, D_MODEL)),
        'w_ple_gate': nrm(ks[20], (DEPTH, D_MODEL, D_MODEL), D_MODEL ** -0.5),
        'w_ple_proj': nrm(ks[21], (DEPTH, PLE_DIM, D_MODEL), PLE_DIM ** -0.5),
        'ple_post_g': gain(ks[22], (DEPTH, D_MODEL)),
    }


def reference(x, p, positions, attn_norm_g, w_in, conv_w, conv_b, dt_bias, a_log, d_skip,
              ssm_norm_g, q_norm_g, k_norm_g, sinks, w_out, ffn_norm_g, w_up, ffn_conv_w,
              ffn_conv_b, w_down, ple_norm_g, w_ple_gate, w_ple_proj, ple_post_g):
    b, s = x.shape[0], x.shape[1]
    inv_freq = ROPE_THETA ** (-jnp.arange(0, ROPE_DIM, 2, dtype=jnp.float32) / ROPE_DIM)
    ang = positions.astype(jnp.float32)[..., None] * inv_freq
    cos = jnp.cos(ang)[:, :, None, :].astype(x.dtype)
    sin = jnp.sin(ang)[:, :, None, :].astype(x.dtype)
    o1 = SSM_INNER
    o2 = o1 + SSM_CONV_CH
    o3 = o2 + SSM_HEADS
    o4 = o3 + ATT_INNER
    o5 = o4 + ATT_KV_DIM
    h = x
    for i in range(DEPTH):
        a_in = _rmsnorm(h, attn_norm_g[i])
        proj = a_in @ w_in[i]
        z, xbc, dt_raw, q, k, v = jnp.split(proj, [o1, o2, o3, o4, o5], axis=-1)
        xbc = jax.nn.silu(_causal_dwconv(xbc, conv_w[i], conv_b[i]))
        xs, bm, cm = jnp.split(xbc, [SSM_INNER, SSM_INNER + SSM_GROUPS * SSM_STATE], axis=-1)
        dt = jax.nn.softplus(dt_raw.astype(jnp.float32) + dt_bias[i].astype(jnp.float32))
        a_neg = -jnp.exp(a_log[i].astype(jnp.float32))
        xs_h = xs.reshape(b, s, SSM_HEADS, SSM_HEAD_DIM)
        y = _ssd(xs_h, dt, a_neg, bm.reshape(b, s, SSM_GROUPS, SSM_STATE),
                 cm.reshape(b, s, SSM_GROUPS, SSM_STATE))
        y = y + d_skip[i].astype(jnp.float32)[:, None] * xs_h.astype(jnp.float32)
        y = y.reshape(b, s, SSM_INNER) * jax.nn.silu(z.astype(jnp.float32))
        y_ssm = _rmsnorm(y.reshape(b, s, SSM_GROUPS, SSM_INNER // SSM_GROUPS),
                         ssm_norm_g[i].reshape(SSM_GROUPS, SSM_INNER // SSM_GROUPS))
        y_ssm = y_ssm.reshape(b, s, SSM_INNER).astype(h.dtype)
        qh = _rmsnorm(q.reshape(b, s, ATT_HEADS, ATT_HEAD_DIM), q_norm_g[i])
        kh = _rmsnorm(k.reshape(b, s, ATT_KV_HEADS, ATT_HEAD_DIM), k_norm_g[i])
        qh = _partial_rope(qh, cos, sin)
        kh = _partial_rope(kh, cos, sin)
        vh = v.reshape(b, s, ATT_KV_HEADS, ATT_HEAD_DIM)
        y_att = _swa_sinks(qh, kh, vh, sinks[i])
        h = h + jnp.concatenate([y_ssm, y_att], axis=-1) @ w_out[i]
        u = _rmsnorm(h, ffn_norm_g[i]) @ w_up[i]
        u = _causal_dwconv(u, ffn_conv_w[i], ffn_conv_b[i])
        g_ff, u_ff = jnp.split(u, [D_FF], axis=-1)
        h = h + (jax.nn.silu(g_ff) * u_ff) @ w_down[i]
        gate = jax.nn.sigmoid(_rmsnorm(h, ple_norm_g[i]) @ w_ple_gate[i])
        pe = _rmsnorm(p[i] @ w_ple_proj[i], ple_post_g[i])
        h = h + gate * pe
    return h
```

```python
import numpy as np
from contextlib import ExitStack
import concourse.bass as bass
import concourse.mybir as mybir
from concourse.bass_utils import run_bass_kernel_spmd

F32, BF16, I32 = mybir.dt.float32, mybir.dt.bfloat16, mybir.dt.int32
AF = mybir.ActivationFunctionType
ALU = mybir.AluOpType

NCORES = 8
D = 2048
SEQ = 8192
TOK = SEQ // NCORES
NSLOT = 8192
NPRE = 55
MAIN0 = NPRE * 128
M10 = MAIN0 - 128
EPS = 1e-6
DFF = 5632
O_Z, O_X, O_B, O_C, O_DT, O_Q, O_K, O_V = 0, 2048, 4096, 4608, 5120, 5152, 7200, 7712
ARENA_BYTES = 212000
LAST_ARENA = None
TWO_PI_HI = 6.28125
TWO_PI_LO = 2.0 * np.pi - 6.28125


class Sem:
    def __init__(self, h):
        self.h = h
        self.count = 0


class Eng:
    def __init__(self, name, h, sem, inorder=False):
        self.name, self.h, self.sem, self.inorder = name, h, sem, inorder
        self.waited = {}


class Buf:
    def __init__(self, name=""):
        self.name = name
        self.w = {}
        self.r = {}
        self.excl = False


def _merge(d, e):
    for k, v in e.items():
        if d.get(k, 0) < v:
            d[k] = v


class T:
    def __init__(self, v, name):
        self.v = v
        self.b = Buf(name)

    def __getitem__(self, k):
        return self.v[k]


class KB:
    def __init__(self, nc, es):
        self.nc, self.es = nc, es
        self.pe = Eng("pe", nc.tensor, self.sem("s_pe"), inorder=True)
        self.act = Eng("act", nc.scalar, self.sem("s_act"))
        self.dve = Eng("dve", nc.vector, self.sem("s_dve"))
        self.pool = Eng("pool", nc.gpsimd, self.sem("s_pool"))
        self.sp = Eng("sp", nc.sync, self.sem("s_sp"))
        self.engs = [self.pe, self.act, self.dve, self.pool, self.sp]
        self.dsems = []
        self.rec = None

    def record(self):
        self.rec = []

    def stop(self):
        r, self.rec = self.rec, None
        return r

    def replay(self, items, n=None):
        n = len(items) if n is None else min(n, len(items))
        for it in items[:n]:
            if it[0] == "op":
                self.op(it[1], it[2], r=it[3], w=it[4], **it[5])
            else:
                self.dma(it[1], it[2], it[3], it[4], r=it[5], w=it[6])
        return items[n:]

    def sem(self, name):
        return Sem(self.es.enter_context(self.nc.semaphore(name)))

    def dsem(self, name):
        s = self.sem(name)
        self.dsems.append(s)
        return s

    def _wait(self, eng, deps):
        for sem, val in deps.items():
            if sem is eng.sem and eng.inorder:
                continue
            if eng.waited.get(sem, 0) < val:
                eng.h.wait_ge(sem.h, val)
                eng.waited[sem] = val

    def _deps(self, r, w):
        deps = {}
        for b in r:
            _merge(deps, b.b.w)
            if b.b.excl:
                _merge(deps, b.b.r)
        for b in w:
            _merge(deps, b.b.w)
            _merge(deps, b.b.r)
        return deps

    def _done(self, ev, r, w):
        for b in w:
            b.b.w = dict(ev)
            b.b.r = {}
        for b in r:
            _merge(b.b.r, ev)

    def op(self, eng, name, r=(), w=(), **kw):
        if self.rec is not None:
            self.rec.append(("op", eng, name, tuple(r), tuple(w), kw))
            return None
        self._wait(eng, self._deps(r, w))
        ins = getattr(eng.h, name)(**kw)
        eng.sem.count += 1
        ins.then_inc(eng.sem.h, 1)
        self._done({eng.sem: eng.sem.count}, r, w)
        return ins

    def dma(self, eng, out, in_, sem, r=(), w=()):
        if self.rec is not None:
            self.rec.append(("dma", eng, out, in_, sem, tuple(r), tuple(w)))
            return None
        self._wait(eng, self._deps(r, w))
        ins = eng.h.dma_start(out=out, in_=in_)
        sem.count += 16
        ins.then_inc(sem.h, 16)
        self._done({sem: sem.count}, r, w)
        return ins

    def barrier(self):
        tot = {}
        for e in self.engs:
            if e.sem.count:
                tot[e.sem] = e.sem.count
        for s in self.dsems:
            if s.count:
                tot[s] = s.count
        for e in self.engs:
            self._wait(e, {k: v for k, v in tot.items() if k is not e.sem})


class Arena:
    def __init__(self, t, nbytes):
        self.t, self.nbytes, self.top = t, nbytes, 0
        self.offs = {}

    def alloc_at(self, name, shape, dt, off):
        save = self.top
        self.top = off
        t = self.alloc(name, shape, dt)
        end = self.top
        self.top = save
        return t, end

    def alloc(self, name, shape, dt):
        P = shape[0]
        free = 1
        for s in shape[1:]:
            free *= s
        esz = 2 if dt == BF16 else 4
        nb = free * esz
        off = self.top
        self.offs[name] = off
        self.top += (nb + 31) // 32 * 32
        self.hw = max(getattr(self, 'hw', 0), self.top)
        assert self.top <= self.nbytes, f"SBUF arena overflow at {name}: {self.top}"
        v = self.t[0:P, off // 2: off // 2 + nb // 2]
        if esz == 4:
            v = v.bitcast(dt)
        if len(shape) > 2:
            names = "abcdef"[:len(shape) - 1]
            kw = {names[i]: shape[i + 1] for i in range(1, len(names))}
            v = v.rearrange(f"p ({' '.join(names)}) -> p {' '.join(names)}", **kw)
        return T(v, name)


class _Cut(Exception):
    pass


def build_nc(stage="full"):
    cutn = None
    if ":cut" in stage:
        stage, c_ = stage.split(":cut")
        cutn = int(c_)

    noprefix = stage.startswith("np_")
    if noprefix:
        stage = stage[3:]

    def cut(n):
        if cutn == n:
            raise _Cut()
    nc = bass.Bass("TRN2", target_bir_lowering=False)
    es = ExitStack()
    kb = KB(nc, es)
    pe, act, dve, pool, sp = kb.pe, kb.act, kb.dve, kb.pool, kb.sp
    global LAST_ARENA
    ar = LAST_ARENA = Arena(es.enter_context(nc.sbuf_tensor("arena", [128, ARENA_BYTES // 2], BF16)), ARENA_BYTES)
    psum = es.enter_context(nc.psum_tensor("psum", [128, 4096], F32))

    def bank(i, name):
        t = T(psum[:, i * 512:(i + 1) * 512], name)
        t.b.excl = True
        return t

    def bank_bf(i, name):
        t = T(psum[:, i * 512:(i + 1) * 512].bitcast(BF16), name)
        t.b.excl = True
        return t

    def din(name, shape, dt=F32):
        return nc.dram_tensor(name, list(shape), dt, kind="ExternalInput").ap()

    def mm(out, lhsT, rhs, start, stop, r, w):
        kb.op(pe, "matmul", r=r, w=w, out=out, lhsT=lhsT, rhs=rhs, start=start, stop=stop)

    def trp(out, in_, idn, r, w):
        kb.op(pe, "transpose", r=r, w=w, out=out, in_=in_, identity=idn)

    def A(out, in_, func, r, w, **kw):
        kb.op(act, "activation", r=r, w=w, out=out, in_=in_, func=func, **kw)

    def TTo(eng, out, in0, in1, op, r, w):
        kb.op(eng, "tensor_tensor", r=r, w=w, out=out, in0=in0, in1=in1, op=op)

    def TS(eng, out, in0, s1, op0, r, w, s2=None, op1=None):
        if op1 is None:
            kb.op(eng, "tensor_scalar", r=r, w=w, out=out, in0=in0, scalar1=s1, scalar2=None, op0=op0)
        else:
            kb.op(eng, "tensor_scalar", r=r, w=w, out=out, in0=in0, scalar1=s1, scalar2=s2, op0=op0, op1=op1)

    def STT(out, in0, scalar, in1, op0, op1, r, w):
        kb.op(dve, "scalar_tensor_tensor", r=r, w=w, out=out, in0=in0, scalar=scalar, in1=in1, op0=op0, op1=op1)

    def CP(eng, out, in_, r, w):
        kb.op(eng, "tensor_copy", r=r, w=w, out=out, in_=in_)

    xT = din("xT", [D, NSLOT]).rearrange("(k p) t -> p k t", p=128)
    msk = din("msk", [32, NSLOT])
    cst = din("cst", [128, 1024])
    cst32 = din("cst32", [32, 768])
    w_in = din("w_in", [D, 8224]).rearrange("(k p) f -> p k f", p=128)
    vecs = din("vecs", [128, 256])
    conv_w = din("conv_w", [128, 24, 4])
    dt_bias = din("dt_bias", [32, 1])
    a_log = din("a_log", [32, 1])

    ident = ar.alloc("ident", [128, 128], BF16)
    identf = ar.alloc("identf", [128, 128], F32)
    Tm = ar.alloc("Tm", [128, 128], BF16)
    Um = ar.alloc("Um", [128, 128], BF16)
    ones = ar.alloc("ones", [128, 128], BF16)
    RmT = ar.alloc("RmT", [128, 128], BF16)
    bdo = ar.alloc("bdo", [128, 128], BF16)
    selH = ar.alloc("selH", [32, 128], F32)
    selB = ar.alloc("selB", [32, 16], F32)
    rstm = ar.alloc("rstm", [32, 512], F32)
    vec = ar.alloc("vec", [128, 256], F32)
    cw = ar.alloc("cw", [128, 24, 4], F32)
    dtb = ar.alloc("dtb", [32, 1], F32)
    acol = ar.alloc("acol", [32, 1], F32)
    ST = ar.alloc("ST", [128, 16, 128], F32)
    V_GA, V_CB, V_DCOL, V_GSSM, V_GQ, V_GK, V_HV, V_INVF = 0, 16, 40, 56, 72, 73, 74, 75
    V_SINK, V_GF, V_GP, V_GPP = 76, 108, 124, 140
    consts_all = [ident, identf, Tm, Um, ones, RmT, bdo, selH, selB, rstm, vec, cw, dtb, acol]

    s_setup = kb.dsem("d_setup")
    s_setup2 = kb.dsem("d_setup2")
    kb.dma(pool, ident[:, :], cst[:, 0:128], s_setup2, w=[ident])
    kb.dma(sp, identf[:, :], cst[:, 0:128], s_setup, w=[identf])
    kb.dma(pool, Tm[:, :], cst[:, 128:256], s_setup2, w=[Tm])
    kb.dma(pool, Um[:, :], cst[:, 256:384], s_setup2, w=[Um])
    kb.dma(pool, ones[:, :], cst[:, 384:512], s_setup2, w=[ones])
    kb.dma(pool, RmT[:, :], cst[:, 512:640], s_setup2, w=[RmT])
    kb.dma(pool, bdo[:, :], cst[:, 640:768], s_setup2, w=[bdo])
    kb.dma(sp, selH[:, :], cst32[:, 0:128], s_setup, w=[selH])
    kb.dma(sp, selB[:, :], cst32[:, 128:144], s_setup, w=[selB])
    kb.dma(sp, rstm[:, :], cst32[:, 256:768], s_setup, w=[rstm])
    kb.dma(sp, vec[:, :], vecs[:, :], s_setup, w=[vec])
    kb.dma(sp, cw[:, :, :], conv_w[:, :, :], s_setup, w=[cw])
    kb.dma(sp, dtb[:, :], dt_bias[:, :], s_setup, w=[dtb])
    kb.dma(sp, acol[:, :], a_log[:, :], s_setup, w=[acol])
    for c_ in consts_all:
        c_.b.w = {s_setup: s_setup.count, s_setup2: s_setup2.count}
    A(acol[:, :], acol[:, :], AF.Exp, [acol], [acol])
    TS(dve, acol[:, :], acol[:, :], -1.0, ALU.mult, [acol], [acol])
    kb.op(dve, "memset", w=[ST], ap=ST[:, :, :], constant=0.0)

    P_TOP = ar.top

    def dt_chain(n, ps_ms, xg_rhs, Wdt, rs, mk, bufs, extra):
        nch = n // 128
        dv, dtm, adt, csT, rem, wT = (bufs[k] for k in ("dv", "dtm", "adt", "csT", "rem", "wT"))
        for k in range(16):
            mm(ps_ms[0:32, 0:n], Wdt[:, k, :], xg_rhs(k), k == 0, k == 15, [Wdt, *extra], [ps_ms])
        TTo(dve, dv[:, 0:n], ps_ms[0:32, 0:n], rs, ALU.mult, [ps_ms, *extra], [dv])
        A(dv[:, 0:n], dv[:, 0:n], AF.Exp, [dv, dtb], [dv], bias=dtb[:, 0:1])
        A(dv[:, 0:n], dv[:, 0:n], AF.Ln, [dv], [dv], bias=1.0)
        TTo(dve, dtm[:, 0:n], dv[:, 0:n], mk[:, 0:n], ALU.mult, [dv, mk], [dtm])
        TS(dve, adt[:, 0:n], dtm[:, 0:n], acol[:, 0:1], ALU.mult, [dtm, acol], [adt])
        kb.op(dve, "tensor_tensor_scan", r=[rstm, adt], w=[csT], out=csT[:, 0:n], data0=rstm[:, 0:n],
              data1=adt[:, 0:n], initial=0.0, op0=ALU.mult, op1=ALU.add)
        cs3 = csT[:, 0:n].rearrange("p (c l) -> p c l", l=128)
        TTo(dve, rem[:, 0:n].rearrange("p (c l) -> p c l", l=128), cs3[:, :, 127:128].broadcast_to([32, nch, 128]),
            cs3, ALU.subtract, [csT], [rem])
        A(rem[:, 0:n], rem[:, 0:n], AF.Exp, [rem], [rem])
        TTo(dve, wT[:, 0:n], rem[:, 0:n], dtm[:, 0:n], ALU.mult, [rem, dtm], [wT])
        if "ecs" in bufs:
            A(bufs["ecs"][:, 0:n], csT[:, 0:n], AF.Exp, [csT], [bufs["ecs"]])

    NT = 512
    DG_OFF = ar.top
    dg = ar.alloc("dg", [128, 96, 128], BF16)
    cbv = lambda blk: vec[:, V_CB + blk:V_CB + blk + 1]
    for blk in range(24):
        for k in range(4):
            TS(dve, dg[:, blk * 4 + k, :], ident[:, :], cw[:, blk, k:k + 1], ALU.mult, [ident, cw], [dg])
    DG_TOP = ar.top
    Wp = ar.alloc("Wp", [128, 16, 2592], BF16)
    hal = ar.alloc("hal", [128, 24, 3], BF16)
    kb.op(dve, "memset", w=[hal], ap=hal[:, :, :], constant=0.0)
    WpS = {}
    for nm_ in ("dt", "B", "x0", "x1", "x2", "x3"):
        t_ = T(Wp.v, "Wp_" + nm_)
        WpS[nm_] = (t_, kb.dsem("d_wp_" + nm_))
    kb.dma(pool, Wp[:, :, 2560:2592], w_in[:, :, O_DT:O_DT + 32], WpS["dt"][1], w=[WpS["dt"][0]])
    kb.dma(pool, Wp[:, :, 2048:2560], w_in[:, :, O_B:O_B + 512], WpS["B"][1], w=[WpS["B"][0]])
    for j in range(4):
        kb.dma(pool, Wp[:, :, j * 512:(j + 1) * 512], w_in[:, :, O_X + j * 512:O_X + (j + 1) * 512], WpS[f"x{j}"][1],
               w=[WpS[f"x{j}"][0]])

    xin = [ar.alloc(f"xin{i}", [128, NT], F32) for i in range(5)]
    s_xin = [kb.dsem(f"d_xin{i}") for i in range(5)]
    sq = [ar.alloc(f"sq{i}", [128, NT], BF16) for i in range(4)]
    xg = [ar.alloc(f"xg{i}", [128, 16, NT], BF16) for i in range(2)]
    rstd = [ar.alloc(f"rstd{i}", [128, NT], F32) for i in range(2)]
    lnv = ar.alloc("lnv", [128, NT], F32)
    mskt = [ar.alloc(f"mskt{i}", [32, NT], F32) for i in range(2)]
    s_msk = [kb.dsem(f"d_msk{i}") for i in range(2)]
    dtb_p = {k: ar.alloc("p_" + k, [32, NT], F32) for k in ("dv", "dtm", "adt", "csT", "rem", "wT")}
    wtok = [ar.alloc(f"wtok{i}", [128, 4, 32], F32) for i in range(2)]
    seltot = ar.alloc("seltot", [32, 4, 16], F32)
    decall = [ar.alloc(f"decall{i}", [128, 4, 16], F32) for i in range(2)]
    pre = [ar.alloc(f"pre{i}", [128, 3 + NT], BF16) for i in range(3)]
    post = [ar.alloc(f"post{i}", [128, NT], BF16) for i in range(3)]
    Btok = [ar.alloc(f"Btok{i}", [128, 4, 512], BF16) for i in range(2)]
    Xs = [ar.alloc(f"Xs{i}", [128, 4, 128], BF16) for i in range(3)]

    ps_ss = bank(0, "ps_ss")
    ps_pj = [bank(1, "ps_pj0"), bank(2, "ps_pj1")]
    ps_cv = [bank(3, "ps_cv0"), bank(4, "ps_cv1")]
    ps_trb = bank_bf(5, "ps_tr")
    ps_tr = T(ps_trb[:, 0:512].rearrange("p (c l) -> p c l", l=128), "ps_tr")
    ps_tr.b = ps_trb.b
    ps_stb = bank(6, "ps_st")
    ps_st = T(ps_stb[:, :].rearrange("p (c l) -> p c l", l=128), "ps_st")
    ps_st.b = ps_stb.b
    ps_ms = bank(7, "ps_ms")

    tiles = []
    t0 = 0
    while t0 < NPRE * 128:
        n = min(NT, NPRE * 128 - t0)
        tiles.append((t0, n))
        t0 += n
    if stage == "prefix_small":
        tiles = tiles[:2]
    if noprefix:
        tiles = []

    cnt = {"xin": 0, "sq": 0, "pre": 0, "post": 0, "pj": 0, "cv": 0, "xs": 0}

    lagq = []
    front_dve = [False]

    def front_k(k, t0, n, xgt):
        xi = cnt["xin"] % 5
        cnt["xin"] += 1
        kb.dma(sp, xin[xi][:, 0:n], xT[:, k, t0:t0 + n], s_xin[xi], w=[xin[xi]])
        sqi = sq[cnt["sq"] % 4]
        cnt["sq"] += 1
        A(sqi[:, 0:n], xin[xi][:, 0:n], AF.Square, [xin[xi]], [sqi])
        if front_dve[0]:
            TS(dve, xgt(k), xin[xi][:, 0:n], vec[:, V_GA + k:V_GA + k + 1], ALU.mult, [xin[xi], vec], [xgt.T])
        else:
            A(xgt(k), xin[xi][:, 0:n], AF.Copy, [xin[xi], vec], [xgt.T], scale=vec[:, V_GA + k:V_GA + k + 1])
        lagq.append((k, n, sqi))
        while lagq and (len(lagq) > 2 or k == 15):
            k_, n_, sq_ = lagq.pop(0)
            mm(ps_ss[:, 0:n_], ones[:, :], sq_[:, 0:n_], k_ == 0, k_ == 15, [ones, sq_], [ps_ss])

    def front_fin(n, rs_out, rsT, scale=1.0 / D):
        A(lnv[:, 0:n], ps_ss[:, 0:n], AF.Ln, [ps_ss], [lnv], scale=scale, bias=EPS)
        A(rs_out, lnv[:, 0:n], AF.Exp, [lnv], [rsT], scale=-0.5)

    class XgView:
        def __init__(self, t, c0=0):
            self.T, self.c0 = t, c0
            self.n = None

        def __call__(self, k):
            return self.T[:, k, self.c0:self.c0 + self.n]

    tailmark = [0]

    def prefix_front(ti):
        t0, n = tiles[ti]
        nch = n // 128
        xgt = XgView(xg[ti % 2])
        xgt.n = n
        rs = rstd[ti % 2]
        mk = mskt[ti % 2]
        kb.dma(sp, mk[:, 0:n], msk[:, t0:t0 + n], s_msk[ti % 2], w=[mk])
        for k in range(16):
            front_k(k, t0, n, xgt)
        front_fin(n, rs[:, 0:n], rs)
        WdtT = T(Wp[:, :, 2560:2592], "Wdt")
        WdtT.b = WpS["dt"][0].b
        dt_chain(n, ps_ms, xgt, WdtT, rs[0:32, 0:n], mk, dtb_p, [xgt.T, rs])
        wT, csT = dtb_p["wT"], dtb_p["csT"]
        wk = wtok[ti % 2]
        if kb.rec is not None:
            tailmark[0] = len(kb.rec)
        for c in range(nch):
            trp(ps_ms[:, c * 32:(c + 1) * 32], wT[:, c * 128:(c + 1) * 128], identf[0:32, 0:32], [wT, identf], [ps_ms])
        CP(dve, wk[:, 0:nch, :], ps_ms[:, 0:nch * 32].rearrange("p (c h) -> p c h", h=32), [ps_ms], [wk])
        dk = decall[ti % 2]
        cs3 = csT[:, 0:n].rearrange("p (c l) -> p c l", l=128)
        TTo(dve, seltot[:, 0:nch, :], selB[:, :].unsqueeze(1).broadcast_to([32, nch, 16]),
            cs3[:, :, 127:128].broadcast_to([32, nch, 16]), ALU.mult, [selB, csT], [seltot])
        mm(ps_ms[:, 128:128 + nch * 16], selH[:, :], seltot[:, 0:nch, :].rearrange("p c b -> p (c b)"), True, True,
           [selH, seltot], [ps_ms])
        A(dk[:, 0:nch, :], ps_ms[:, 128:128 + nch * 16].rearrange("p (c b) -> p c b", b=16), AF.Exp, [ps_ms], [dk])


    pend = []
    if tiles:
        prefix_front(0)
    for ti, (t0, n) in enumerate(tiles):
        nch = n // 128
        xgt = XgView(xg[ti % 2])
        xgt.n = n
        rs = rstd[ti % 2]
        wk = wtok[ti % 2]
        dk = decall[ti % 2]
        if ti + 1 < len(tiles):
            kb.record()
            prefix_front(ti + 1)
            pend = kb.stop()
        nhead = tailmark[0] if pend else 0
        sched = [0] * 24
        for bi_ in range(14):
            sched[bi_] = (nhead * (bi_ + 1)) // 14 - (nhead * bi_) // 14
        ntail = len(pend) - nhead
        for bi_ in range(4):
            sched[16 + bi_] = (ntail * (bi_ + 1)) // 4 - (ntail * bi_) // 4
        bt = Btok[ti % 2]
        ctx = {}

        def S1(bi):
            isB = bi < 4
            col0 = (2048 + bi * 128) if isB else (bi - 4) * 128
            cblk = (16 + bi) if isB else (bi - 4)
            pj = ps_pj[cnt["pj"] % 2]
            cnt["pj"] += 1
            wslab_ = WpS["B" if isB else f"x{(bi - 4) // 4}"][0]
            for k in range(16):
                mm(pj[:, 0:n], Wp[:, k, col0:col0 + 128], xgt(k), k == 0, k == 15, [wslab_, xgt.T], [pj])
            pr = pre[cnt["pre"] % 3]
            cnt["pre"] += 1
            TTo(dve, pr[:, 3:3 + n], pj[:, 0:n], rs[:, 0:n], ALU.mult, [pj, rs], [pr])
            CP(dve, pr[:, 0:3], hal[:, cblk, :], [hal], [pr])
            CP(dve, hal[:, cblk, :], pr[:, n:n + 3], [pr], [hal])
            ctx[bi] = {"pr": pr, "cblk": cblk}

        def S2(bi):
            c_ = ctx[bi]
            pr, cblk = c_["pr"], c_["cblk"]
            pc = ps_cv[cnt["cv"] % 2]
            cnt["cv"] += 1
            for k in range(4):
                mm(pc[:, 0:n], dg[:, cblk * 4 + k, :], pr[:, k:k + n], k == 0, k == 3, [dg, pr], [pc])
            po = post[cnt["post"] % 3]
            cnt["post"] += 1
            A(po[:, 0:n], pc[:, 0:n], AF.Silu, [pc, vec], [po], bias=cbv(cblk))
            c_["po"] = po

        def S3(bi):
            c_ = ctx[bi]
            po = c_["po"]
            for c in range(nch):
                trp(ps_tr[:, c, :], po[:, c * 128:(c + 1) * 128], ident[:, :], [po, ident], [ps_tr])
            if bi < 4:
                A(bt[:, 0:nch, bi * 128:(bi + 1) * 128], ps_tr[:, 0:nch, :], AF.Copy, [ps_tr], [bt])
            else:
                b = bi - 4
                xs = Xs[cnt["xs"] % 3]
                cnt["xs"] += 1
                TTo(dve, xs[:, 0:nch, :].rearrange("p c (h d) -> p c h d", d=64),
                    ps_tr[:, 0:nch, :].rearrange("p c (h d) -> p c h d", d=64),
                    wk[:, 0:nch, 2 * b:2 * b + 2].unsqueeze(3).broadcast_to([128, nch, 2, 64]), ALU.mult,
                    [ps_tr, wk], [xs])
                c_["xs"] = xs

        def S4(bi):
            if bi < 4:
                return
            b = bi - 4
            xs = ctx[bi]["xs"]
            g = b // 4
            for c in range(nch):
                mm(ps_st[:, c, :], xs[:, c, :], bt[:, c, g * 128:(g + 1) * 128], True, True, [xs, bt], [ps_st])
            for c in range(nch):
                STT(ST[:, b, :], ST[:, b, :], dk[:, c, b:b + 1], ps_st[:, c, :], ALU.mult, ALU.add,
                    [ST, dk, ps_st], [ST])

        for i in range(20 + 3):
            if i < 20:
                S1(i)
            pend = kb.replay(pend, sched[i])
            if 0 <= i - 1 < 20:
                S2(i - 1)
            if 0 <= i - 2 < 20:
                S3(i - 2)
            if 0 <= i - 3 < 20:
                S4(i - 3)
        pend = kb.replay(pend)

    if stage.startswith("prefix"):
        dbg = nc.dram_tensor("dbg", [128, 16 * 128], F32, kind="ExternalOutput").ap()
        s_out = kb.dsem("d_out")
        kb.dma(sp, dbg[:, :], ST[:, :, :].rearrange("p b n -> p (b n)"), s_out, r=[ST])
        sp.h.wait_ge(s_out.h, s_out.count)
        es.close()
        return nc

    def _main_pass():
        nonlocal xin, sq, lnv, ps_ss, ps_pj, ps_cv, ps_trb, ps_tr, ps_ms, cnt
        kb.barrier()
        ar.top = DG_TOP
        xg10 = ar.alloc("xg10", [128, 16, 1280], BF16)
        rstd10 = ar.alloc("rstd10", [128, 1280], F32)
        MA_TOP = ar.top
        xin = [ar.alloc(f"xin{i}", [128, NT], F32) for i in range(5)]
        sq = [ar.alloc(f"sq{i}", [128, NT], BF16) for i in range(4)]
        lnv = ar.alloc("lnv", [128, NT], F32)
        front_dve[0] = True
        for (c0, n) in ((0, 512), (512, 512), (1024, 256)):
            xgt = XgView(xg10, c0)
            xgt.n = n
            for k in range(16):
                front_k(k, M10 + c0, n, xgt)
            front_fin(n, rstd10[:, c0:c0 + n], rstd10)
        kb.barrier()
        ar.top = MA_TOP
        cut(1)

        ycs = ar.alloc("ycs", [128, 16, 1152], BF16)
        MB_TOP = ar.top
        NM = 1152
        tk = ar.alloc("tk", [128, 9, 4, 32], F32)
        decb = ar.alloc("decb", [128, 9, 32], F32)
        MB2_TOP = ar.top
        onesf = ar.alloc("onesf", [128, 128], F32)
        Wdt = ar.alloc("Wdt", [128, 16, 32], BF16)
        s_wdt = kb.dsem("d_wdt")
        kb.dma(pool, Wdt[:, :, :], w_in[:, :, O_DT:O_DT + 32], s_wdt, w=[Wdt])
        msk9 = ar.alloc("msk9", [32, 1152], F32)
        s_m9 = kb.dsem("d_m9")
        kb.dma(sp, msk9[:, :], msk[:, MAIN0:MAIN0 + 1152], s_m9, w=[msk9])
        dtm_b = {k: ar.alloc("m_" + k, [32, 384], F32) for k in ("dv", "dtm", "adt", "csT", "rem", "wT", "ecs")}
        CP(dve, onesf[:, :], ones[:, :], [ones], [onesf])
        ps_ms = bank(7, "ps_ms")
        for tt in range(3):
            c0 = 128 + tt * 384
            xgt = XgView(xg10, c0)
            xgt.n = 384
            mkv = T(msk9[:, tt * 384:(tt + 1) * 384], "mkv")
            mkv.b = msk9.b
            dt_chain(384, ps_ms, xgt, Wdt, rstd10[0:32, c0:c0 + 384], mkv, dtm_b, [xg10, rstd10])
            for c in range(3):
                for qi, nm in enumerate(("dtm", "adt", "wT", "ecs")):
                    trp(ps_ms[:, (c * 4 + qi) * 32:(c * 4 + qi + 1) * 32], dtm_b[nm][:, c * 128:(c + 1) * 128],
                        identf[0:32, 0:32], [dtm_b[nm], identf], [ps_ms])
            CP(dve, tk[:, tt * 3:(tt + 1) * 3, :, :],
               ps_ms[:, 0:384].rearrange("p (c q h) -> p c q h", q=4, h=32), [ps_ms], [tk])
            for c in range(3):
                mm(ps_ms[:, 384 + c * 32:384 + (c + 1) * 32], onesf[:, :], tk[:, tt * 3 + c, 1, :], True, True,
                   [onesf, tk], [ps_ms])
            A(decb[:, tt * 3:(tt + 1) * 3, :], ps_ms[:, 384:480].rearrange("p (c h) -> p c h", h=32), AF.Exp,
              [ps_ms], [decb])

        cut(2)
        kb.barrier()
        ar.top = MB2_TOP
        dgm, o_ = ar.alloc_at("dgm", [128, 24, 128], BF16, DG_OFF)
        zt = []
        szb = []
        gsq = []
        for i_ in range(2):
            t_, o_ = ar.alloc_at(f"zt{i_}", [128, 384], F32, o_)
            zt.append(t_)
            t_, o_ = ar.alloc_at(f"szb{i_}", [128, 384], F32, o_)
            szb.append(t_)
            t_, o_ = ar.alloc_at(f"gsq{i_}", [128, 384], BF16, o_)
            gsq.append(t_)
        gacc, o_ = ar.alloc_at("gacc", [128, 1152], F32, o_)
        assert o_ <= DG_OFF + 24576, o_
        wsl = [ar.alloc(f"wsl{i}", [128, 16, 256], BF16) for i in range(3)]
        s_wsl = [kb.dsem(f"d_wsl{i}") for i in range(3)]
        wcnt = [0]

        def wload(src):
            i = wcnt[0] % 3
            wcnt[0] += 1
            kb.dma(pool, wsl[i][:, :, 0:src.shape[2]], src, s_wsl[i], w=[wsl[i]])
            return wsl[i]

        pre9 = [ar.alloc(f"pre9_{i}", [128, 1155], BF16) for i in range(2)]
        XTg = ar.alloc("XTg", [128, 4, NM], BF16)
        BTg = ar.alloc("BTg", [128, NM], BF16)
        CTg = ar.alloc("CTg", [128, NM], BF16)
        Btk = ar.alloc("Btk", [128, 9, 128], BF16)
        Xtk = ar.alloc("Xtk", [128, 9, 512], BF16)
        ytk = ar.alloc("ytk", [128, 9, 512], BF16)
        Sg = ar.alloc("Sg", [128, 512], F32)
        Sbf = ar.alloc("Sbf", [128, 512], BF16)
        CBm2 = [ar.alloc(f"CBm{i}", [128, 128], BF16) for i in range(2)]
        Rb2 = [ar.alloc(f"Rb{i}", [128, 8, 128], BF16) for i in range(2)]
        Eb = [ar.alloc(f"Eb{i}", [128, 4, 128], BF16) for i in range(2)]
        MTb = [ar.alloc(f"MTb{i}", [128, 4, 128], BF16) for i in range(2)]
        Xdt2 = [ar.alloc(f"Xdt{i}", [128, 512], BF16) for i in range(2)]
        Xw = ar.alloc("Xw", [128, 512], BF16)
        t1 = ar.alloc("t1", [128, 512], F32)
        ps_pj = [bank(0, "ps_pj0"), bank(1, "ps_pj1")]
        ps_cv = bank(2, "ps_cv")
        ps_trb = bank_bf(3, "ps_tr")
        ps_seg = bank(4, "ps_seg")
        ps_y2 = [bank(5, "ps_y0"), bank(7, "ps_y1")]
        ps_yo = bank(6, "ps_yo")
        ps_cv2 = [ps_cv, ps_seg]
        ps_trb2 = T(psum[:, 5 * 512:6 * 512].bitcast(BF16), "ps_trb2")
        ps_trb2.b = ps_y2[0].b
        trs = []
        for t_ in (ps_trb, ps_trb2):
            v_ = T(t_[:, 0:512].rearrange("p (c l) -> p c l", l=128), "ps_trx")
            v_.b = t_.b
            trs.append(v_)
        trc = [0]

        def next_tr():
            trc[0] += 1
            return trs[trc[0] % 2]
        pcnt = [0]

        def proj_block_main(wslab, wc0, dst_pre):
            for tt in range(3):
                c0 = 125 + tt * 385
                pj = ps_pj[pcnt[0] % 2]
                pcnt[0] += 1
                for k in range(16):
                    mm(pj[:, 0:385], wslab[:, k, wc0:wc0 + 128], xg10[:, k, c0:c0 + 385], k == 0, k == 15,
                       [wslab, xg10], [pj])
                TTo(dve, dst_pre[:, tt * 385:(tt + 1) * 385], pj[:, 0:385], rstd10[:, c0:c0 + 385], ALU.mult,
                    [pj, rstd10], [dst_pre])

        cvc = [0]

        def conv_main(cblk, li, src_pre, dstT, dst_ap):
            for tt in range(3):
                pcv = ps_cv2[cvc[0] % 2]
                cvc[0] += 1
                for k in range(4):
                    mm(pcv[:, 0:384], dgm[:, li * 4 + k, :], src_pre[:, tt * 384 + k:tt * 384 + k + 384], k == 0, k == 3,
                       [dgm, src_pre], [pcv])
                A(dst_ap(tt), pcv[:, 0:384], AF.Silu, [pcv, vec], [dstT], bias=cbv(cblk))

        ps_sq = []
        for t_ in (ps_cv, ps_trb):
            v_ = T(psum[:, (2 if t_ is ps_cv else 3) * 512:(3 if t_ is ps_cv else 4) * 512], "ps_sq")
            v_.b = t_.b
            ps_sq.append(v_)
        zc = [0]
        w_in_z = lambda j_: w_in[:, :, O_Z + j_ * 256:O_Z + (j_ + 1) * 256]

        def zgate_group(g):
            glag = []

            def gn_step(tt_, qq_, first):
                pq = ps_sq[zc[0] % 2]
                cols_ = slice(tt_ * 384, (tt_ + 1) * 384)
                mm(pq[:, 0:384], ones[:, :], qq_[:, :], True, True, [ones, qq_], [pq])
                if first:
                    CP(dve, gacc[:, cols_], pq[:, 0:384], [pq], [gacc])
                else:
                    TTo(dve, gacc[:, cols_], gacc[:, cols_], pq[:, 0:384], ALU.add, [gacc, pq], [gacc])

            for sl in range(2):
                cur = wload(w_in_z(2 * g + sl))
                for bb in range(2):
                    blk = 4 * g + 2 * sl + bb
                    for tt in range(3):
                        c0 = 128 + tt * 384
                        cols = slice(tt * 384, (tt + 1) * 384)
                        pj = ps_pj[pcnt[0] % 2]
                        pcnt[0] += 1
                        for k in range(16):
                            mm(pj[:, 0:384], cur[:, k, bb * 128:(bb + 1) * 128], xg10[:, k, c0:c0 + 384], k == 0, k == 15,
                               [cur, xg10], [pj])
                        z_ = zt[zc[0] % 2]
                        s_ = szb[zc[0] % 2]
                        q_ = gsq[zc[0] % 2]
                        zc[0] += 1
                        TTo(dve, z_[:, :], pj[:, 0:384], rstd10[:, c0:c0 + 384], ALU.mult, [pj, rstd10], [z_])
                        A(s_[:, :], z_[:, :], AF.Silu, [z_], [s_])
                        TTo(dve, ycs[:, blk, cols], ycs[:, blk, cols], s_[:, :], ALU.mult, [ycs, s_], [ycs])
                        A(q_[:, :], ycs[:, blk, cols], AF.Square, [ycs], [q_])
                        glag.append((tt, q_, blk % 4 == 0))
                        if len(glag) > 1:
                            gn_step(*glag.pop(0))
            while glag:
                gn_step(*glag.pop(0))
            A(gacc[:, :], gacc[:, :], AF.Ln, [gacc], [gacc], scale=1.0 / 512, bias=EPS)
            A(gacc[:, :], gacc[:, :], AF.Exp, [gacc], [gacc], scale=-0.5)
            for b4 in range(4):
                bk = g * 4 + b4
                STT(ycs[:, bk, :], ycs[:, bk, :], vec[:, V_GSSM + bk:V_GSSM + bk + 1], gacc[:, :], ALU.mult, ALU.mult,
                    [ycs, vec, gacc], [ycs])

        ngroups = 1 if stage == "ssd_g0" else 4
        for g in range(4):
            if g >= ngroups:
                break
            slab_bc = wload(w_in[:, :, O_B + g * 128:O_B + (g + 1) * 128])
            slab_c = wload(w_in[:, :, O_C + g * 128:O_C + (g + 1) * 128])
            slab_x = [wload(w_in[:, :, O_X + g * 512:O_X + g * 512 + 256])]
            jobs = [(slab_bc, 0, 16 + g, BTg, lambda tt: BTg[:, tt * 384:(tt + 1) * 384]),
                    (slab_c, 0, 20 + g, CTg, lambda tt: CTg[:, tt * 384:(tt + 1) * 384])]
            for i in range(4):
                jobs.append((None, (i % 2) * 128, 4 * g + i, XTg, lambda tt, i=i: XTg[:, i, tt * 384:(tt + 1) * 384]))
            for li, (_s, _w, cblk_, _d, _a) in enumerate(jobs):
                for k in range(4):
                    TS(dve, dgm[:, li * 4 + k, :], ident[:, :], cw[:, cblk_, k:k + 1], ALU.mult, [ident, cw], [dgm])
            prev = None
            for ji, (slb, wc0, cblk, dstT, dst_ap) in enumerate(jobs):
                if ji == 2:
                    slab_x.append(wload(w_in[:, :, O_X + g * 512 + 256:O_X + g * 512 + 512]))
                if slb is None:
                    slb = slab_x[(ji - 2) // 2]
                pr = pre9[ji % 2]
                proj_block_main(slb, wc0, pr)
                if prev is not None:
                    conv_main(*prev)
                prev = (cblk, ji, pr, dstT, dst_ap)
            conv_main(*prev)
            cut(3)
            for c3 in range(3):
                ps_tr = next_tr()
                for c in range(3):
                    trp(ps_tr[:, c, :], BTg[:, (c3 * 3 + c) * 128:(c3 * 3 + c + 1) * 128], ident[:, :], [BTg, ident], [ps_tr])
                A(Btk[:, c3 * 3:c3 * 3 + 3, :], ps_tr[:, 0:3, :], AF.Copy, [ps_tr], [Btk])
            for i in range(4):
                for c3 in range(3):
                    ps_tr = next_tr()
                    for c in range(3):
                        trp(ps_tr[:, c, :], XTg[:, i, (c3 * 3 + c) * 128:(c3 * 3 + c + 1) * 128], ident[:, :],
                            [XTg, ident], [ps_tr])
                    if trc[0] % 2:
                        A(Xtk[:, c3 * 3:c3 * 3 + 3, i * 128:(i + 1) * 128], ps_tr[:, 0:3, :], AF.Copy, [ps_tr], [Xtk])
                    else:
                        CP(dve, Xtk[:, c3 * 3:c3 * 3 + 3, i * 128:(i + 1) * 128], ps_tr[:, 0:3, :], [ps_tr], [Xtk])
            cut(4)
            for i in range(4):
                mm(ps_yo[:, i * 128:(i + 1) * 128], ST[:, 4 * g + i, :], identf[:, :], True, True, [ST, identf], [ps_yo])
            CP(dve, Sg[:, :], ps_yo[:, :], [ps_yo], [Sg])
            A(Sbf[:, :], ps_yo[:, :], AF.Copy, [ps_yo], [Sbf])
            cut(5)
            hs = slice(g * 8, g * 8 + 8)

            def chunkA(c):
                cs_ = slice(c * 128, (c + 1) * 128)
                cbm, rb, xdt, psy = CBm2[c % 2], Rb2[c % 2], Xdt2[c % 2], ps_y2[c % 2]
                mm(ps_seg[:, 0:128], BTg[:, cs_], CTg[:, cs_], True, True, [BTg, CTg], [ps_seg])
                TTo(dve, cbm[:, :], ps_seg[:, 0:128], Tm[:, :], ALU.mult, [ps_seg, Tm], [cbm])
                TTo(pool, rb[:, :, :], tk[:, c, 1, hs].unsqueeze(2).broadcast_to([128, 8, 128]),
                    Tm[:, :].unsqueeze(1).broadcast_to([128, 8, 128]), ALU.mult, [tk, Tm], [rb])
                TTo(pool, xdt[:, :].rearrange("p (h d) -> p h d", d=64), Xtk[:, c, :].rearrange("p (h d) -> p h d", d=64),
                    tk[:, c, 0, hs].unsqueeze(2).broadcast_to([128, 8, 64]), ALU.mult, [Xtk, tk], [xdt])
                for hf in range(2):
                    mm(ps_seg[:, :], Um[:, :], rb[:, hf * 4:(hf + 1) * 4, :].rearrange("p h l -> p (h l)"), True, True,
                       [Um, rb], [ps_seg])
                    E = Eb[hf]
                    A(E[:, :, :], ps_seg[:, :].rearrange("p (h l) -> p h l", l=128), AF.Exp, [ps_seg], [E])
                    MT = MTb[hf]
                    TTo(dve, MT[:, :, :], E[:, :, :], cbm[:, :].unsqueeze(1).broadcast_to([128, 4, 128]), ALU.mult,
                        [E, cbm], [MT])
                    for hh in range(4):
                        o = (hf * 4 + hh) * 64
                        mm(psy[:, o:o + 64], MT[:, hh, :], xdt[:, o:o + 64], True, True, [MT, xdt], [psy])

            def chunkB(c):
                cs_ = slice(c * 128, (c + 1) * 128)
                psy = ps_y2[c % 2]
                mm(ps_yo[:, :], CTg[:, cs_], Sbf[:, :], True, True, [CTg, Sbf], [ps_yo])
                TTo(dve, t1[:, :].rearrange("p (h d) -> p h d", d=64), ps_yo[:, :].rearrange("p (h d) -> p h d", d=64),
                    tk[:, c, 3, hs].unsqueeze(2).broadcast_to([128, 8, 64]), ALU.mult, [ps_yo, tk], [t1])
                TTo(dve, ytk[:, c, :], psy[:, :], t1[:, :], ALU.add, [psy, t1], [ytk])
                if c < 8:
                    TTo(pool, Xw[:, :].rearrange("p (h d) -> p h d", d=64), Xtk[:, c, :].rearrange("p (h d) -> p h d", d=64),
                        tk[:, c, 2, hs].unsqueeze(2).broadcast_to([128, 8, 64]), ALU.mult, [Xtk, tk], [Xw])
                    mm(ps_yo[:, :], Btk[:, c, :], Xw[:, :], True, True, [Btk, Xw], [ps_yo])
                    TTo(dve, Sg[:, :].rearrange("p (h d) -> p h d", d=64), Sg[:, :].rearrange("p (h d) -> p h d", d=64),
                        decb[:, c, hs].unsqueeze(2).broadcast_to([128, 8, 64]), ALU.mult, [Sg, decb], [Sg])
                    TTo(dve, Sg[:, :], Sg[:, :], ps_yo[:, :], ALU.add, [Sg, ps_yo], [Sg])
                    A(Sbf[:, :], Sg[:, :], AF.Copy, [Sg], [Sbf])

            zp = []
            if g >= 1 and stage not in ("ssd", "ssd_g0"):
                kb.record()
                zgate_group(g - 1)
                zp = kb.stop()
            zburst = (len(zp) + 2) // 3
            chunkA(0)
            for c in range(9):
                if c + 1 < 9:
                    chunkA(c + 1)
                chunkB(c)
                if c in (1, 4, 7):
                    zp = kb.replay(zp, zburst)
            zp = kb.replay(zp)
            cut(6)
            for i in range(4):
                for c3 in range(3):
                    ps_tr = next_tr()
                    for c in range(3):
                        trp(ps_tr[:, c, :], ytk[:, c3 * 3 + c, i * 128:(i + 1) * 128], ident[:, :], [ytk, ident], [ps_tr])
                    STT(ycs[:, 4 * g + i, c3 * 384:(c3 + 1) * 384], XTg[:, i, c3 * 384:(c3 + 1) * 384],
                        vec[:, V_DCOL + 4 * g + i:V_DCOL + 4 * g + i + 1],
                        ps_tr[:, 0:3, :].rearrange("p c l -> p (c l)"), ALU.mult, ALU.add, [XTg, vec, ps_tr], [ycs])

        def dump_bf(t3, nblk, ncol):
            dbg = nc.dram_tensor("dbg", [128, nblk * ncol], BF16, kind="ExternalOutput").ap()
            s_out = kb.dsem("d_out")
            kb.dma(sp, dbg[:, :], t3[:, :, :].rearrange("p b n -> p (b n)"), s_out, r=[t3])
            sp.h.wait_ge(s_out.h, s_out.count)

        if stage in ("ssd", "ssd_g0"):
            dump_bf(ycs, 16, 1152)
            return
        zgate_group(3)
        XG_OFF = ar.offs["xg10"]
        if stage == "gate":
            dump_bf(ycs, 16, 1152)
            return

        kb.barrier()
        ar.top = MB_TOP
        wsl = []
        o_ = DG_OFF
        for i_ in range(3):
            t_, o_ = ar.alloc_at(f"wslb{i_}", [128, 16, 256], BF16, o_)
            wsl.append(t_)
        yca = ar.alloc("yca", [128, 16, 1152], BF16)
        ATT_TOP0 = ar.top
        MB_TOP0 = ar.offs["ycs"]
        pos_i = din("pos", [128, 1280], I32)
        SINt = ar.alloc("SINt", [128, 1280], BF16)
        COSt = ar.alloc("COSt", [128, 1280], BF16)
        es_t = ar.alloc("es_t", [128, 32], F32)
        ones_hv = ar.alloc("ones_hv", [128, 128], BF16)
        hvc = vec[:, V_HV:V_HV + 1]
        A(es_t[:, :], vec[:, V_SINK:V_SINK + 32], AF.Exp, [vec], [es_t])
        TS(dve, ones_hv[:, :], ones[:, :], hvc, ALU.mult, [ones, vec], [ones_hv])
        ATT_TOP = ar.top
        posi = ar.alloc("posi", [128, 1280], I32)
        ang = ar.alloc("ang", [128, 1280], F32)
        kf = ar.alloc("kf", [128, 1280], F32)
        ki = ar.alloc("ki", [128, 1280], I32)
        rr_ = ar.alloc("rr_", [128, 1280], F32)
        m_ = ar.alloc("m_", [128, 1280], F32)
        s_pos = kb.dsem("d_pos")
        kb.dma(sp, posi[:, :], pos_i[:, :], s_pos, w=[posi])
        CP(dve, ang[:, :], posi[:, :], [posi], [ang])
        TS(dve, ang[:, :], ang[:, :], vec[:, V_INVF:V_INVF + 1], ALU.mult, [ang, vec], [ang])
        PI = float(np.pi)
        for which, dst in (("sin", SINt), ("cos", COSt)):
            if which == "cos":
                TS(dve, ang[:, :], ang[:, :], PI / 2, ALU.add, [ang], [ang])
            TS(dve, kf[:, :], ang[:, :], 1.0 / (2 * PI), ALU.mult, [ang], [kf])
            CP(dve, ki[:, :], kf[:, :], [kf], [ki])
            CP(dve, kf[:, :], ki[:, :], [ki], [kf])
            STT(rr_[:, :], kf[:, :], -TWO_PI_HI, ang[:, :], ALU.mult, ALU.add, [kf, ang], [rr_])
            STT(rr_[:, :], kf[:, :], -TWO_PI_LO, rr_[:, :], ALU.mult, ALU.add, [kf, rr_], [rr_])
            TS(dve, m_[:, :], rr_[:, :], PI, ALU.is_gt, [rr_], [m_], s2=-2 * PI, op1=ALU.mult)
            TTo(dve, rr_[:, :], rr_[:, :], m_[:, :], ALU.add, [rr_, m_], [rr_])
            TS(dve, m_[:, :], rr_[:, :], -PI, ALU.is_lt, [rr_], [m_], s2=2 * PI, op1=ALU.mult)
            TTo(dve, rr_[:, :], rr_[:, :], m_[:, :], ALU.add, [rr_, m_], [rr_])
            TS(dve, rr_[:, :], rr_[:, :], PI, ALU.min, [rr_], [rr_], s2=-PI, op1=ALU.max)
            A(dst[:, :], rr_[:, :], AF.Sin, [rr_], [dst])
        cut(10)
        kb.barrier()
        ar.top = ATT_TOP
        QTj = [ar.alloc(f"QTj{i}", [128, 2, 1152], BF16) for i in range(2)]
        KTj = [ar.alloc(f"KTj{i}", [128, 1280], BF16) for i in range(2)]
        Vtj = [ar.alloc(f"Vtj{i}", [128, 10, 128], BF16) for i in range(2)]
        tq = [ar.alloc(f"tq{i}", [128, 512], F32) for i in range(2)]
        sqq = [ar.alloc(f"sqq{i}", [128, 512], BF16) for i in range(2)]
        uq = [ar.alloc(f"uq{i}", [128, 512], BF16) for i in range(2)]
        aq = [ar.alloc(f"aq{i}", [128, 512], F32) for i in range(2)]
        bq = ar.alloc("bq", [128, 512], F32)
        rqq = ar.alloc("rqq", [128, 512], F32)
        vtb = ar.alloc("vtb", [128, 512], BF16)
        Et = [ar.alloc(f"Et{i}", [128, 2, 2, 2, 128], BF16) for i in range(3)]
        rrt = [ar.alloc(f"rrt{i}", [128, 2, 2, 128], F32) for i in range(2)]
        ps_pj = [bank(0, "ps_pj0"), bank(1, "ps_pj1")]
        ps_a = bank(2, "ps_a")
        ps_b = bank(7, "ps_b")
        ps_trb = T(psum[:, 2 * 512:3 * 512].bitcast(BF16), "ps_tr")
        ps_trb.b = ps_a.b
        ps_sc = (bank(3, "ps_s0"), bank(5, "ps_s1"))
        ps_r = bank(4, "ps_r")
        ps_o = bank(6, "ps_o")
        qc = [0]


        qlag = []

        def qk_flush(keep=0):
            while len(qlag) > keep:
                qlag.pop(0)()

        def qk_block(wslab, wc0, gcol, tiles_, tok0, dst_ap, dstT):
            for (o, n) in tiles_:
                c0 = tok0 + o
                pj = ps_pj[pcnt[0] % 2]
                pcnt[0] += 1
                for k in range(16):
                    mm(pj[:, 0:n], wslab[:, k, wc0:wc0 + 128], xg10[:, k, c0:c0 + n], k == 0, k == 15, [wslab, xg10], [pj])
                i_ = qc[0] % 2
                qc[0] += 1
                t_, s_, u_, a_ = tq[i_], sqq[i_], uq[i_], aq[i_]
                TTo(dve, t_[:, 0:n], pj[:, 0:n], rstd10[:, c0:c0 + n], ALU.mult, [pj, rstd10], [t_])
                A(s_[:, 0:n], t_[:, 0:n], AF.Square, [t_], [s_])
                A(u_[:, 0:n], t_[:, 0:n], AF.Copy, [t_, vec], [u_], scale=gcol)

                def tail(o=o, n=n, c0=c0, s_=s_, u_=u_, a_=a_, dst_ap=dst_ap, dstT=dstT):
                    mm(ps_a[:, 0:n], bdo[:, :], s_[:, 0:n], True, True, [bdo, s_], [ps_a])
                    mm(ps_b[:, 0:n], RmT[:, :], u_[:, 0:n], True, True, [RmT, u_], [ps_b])
                    A(rqq[:, 0:n], ps_a[:, 0:n], AF.Ln, [ps_a], [rqq], scale=1.0 / 64, bias=EPS)
                    A(rqq[:, 0:n], rqq[:, 0:n], AF.Exp, [rqq], [rqq], scale=-0.5)
                    TTo(dve, a_[:, 0:n], u_[:, 0:n], COSt[:, c0:c0 + n], ALU.mult, [u_, COSt], [a_])
                    TTo(dve, bq[:, 0:n], ps_b[:, 0:n], SINt[:, c0:c0 + n], ALU.mult, [ps_b, SINt], [bq])
                    TTo(dve, bq[:, 0:n], bq[:, 0:n], a_[:, 0:n], ALU.add, [bq, a_], [bq])
                    TTo(dve, dst_ap(o, n), bq[:, 0:n], rqq[:, 0:n], ALU.mult, [bq, rqq], [dstT])

                qlag.append(tail)
                qk_flush(keep=1)


        T9 = [(0, 384), (384, 384), (768, 384)]
        T10q = [(0, 384), (384, 384), (768, 512)]
        T10 = [(0, 384), (384, 384), (768, 512)]
        SCALE = 0.125
        ec = [0]
        rc = [0]
        nkv = 1 if stage == "att1" else 8
        ps_tr = T(ps_trb[:, 0:512].rearrange("p (c l) -> p c l", l=128), "ps_tr")
        ps_tr.b = ps_trb.b

        def kv_proj(j):
            QT_, KT_, Vt_ = QTj[j % 2], KTj[j % 2], Vtj[j % 2]
            slab_q = wload(w_in[:, :, O_Q + j * 256:O_Q + (j + 1) * 256])
            i_ = wcnt[0] % 3
            wcnt[0] += 1
            slab_kv = wsl[i_]
            kb.dma(pool, slab_kv[:, :, 0:64], w_in[:, :, O_K + j * 64:O_K + (j + 1) * 64], s_wsl[i_], w=[slab_kv])
            kb.dma(pool, slab_kv[:, :, 64:128], w_in[:, :, O_K + j * 64:O_K + (j + 1) * 64], s_wsl[i_], w=[slab_kv])
            kb.dma(pool, slab_kv[:, :, 128:192], w_in[:, :, O_V + j * 64:O_V + (j + 1) * 64], s_wsl[i_], w=[slab_kv])
            kb.dma(pool, slab_kv[:, :, 192:256], w_in[:, :, O_V + j * 64:O_V + (j + 1) * 64], s_wsl[i_], w=[slab_kv])
            for qb in range(2):
                qk_block(slab_q, qb * 128, vec[:, V_GQ:V_GQ + 1], T9, 128,
                         lambda o, n, qb=qb: QT_[:, qb, o:o + n], QT_)
            qk_block(slab_kv, 0, vec[:, V_GK:V_GK + 1], T10q, 0, lambda o, n: KT_[:, o:o + n], KT_)
            first_v = True
            for (o, n) in T10:
                pj = ps_pj[pcnt[0] % 2]
                pcnt[0] += 1
                for k in range(16):
                    mm(pj[:, 0:n], slab_kv[:, k, 128:256], xg10[:, k, o:o + n], k == 0, k == 15, [slab_kv, xg10], [pj])
                TTo(dve, vtb[:, 0:n], pj[:, 0:n], rstd10[:, o:o + n], ALU.mult, [pj, rstd10], [vtb])
                if first_v:
                    qk_flush()
                    first_v = False
                nch = n // 128
                for c in range(nch):
                    trp(ps_tr[:, c, :], vtb[:, c * 128:(c + 1) * 128], ident[:, :], [vtb, ident], [ps_tr])
                A(Vt_[:, o // 128:o // 128 + nch, :], ps_tr[:, 0:nch, :], AF.Copy, [ps_tr], [Vt_])
            TS(dve, Vt_[:, 0:2, :], Vt_[:, 0:2, :], hvc, ALU.mult, [Vt_, vec], [Vt_])

        def att_scores(j, c):
            QT_, KT_ = QTj[j % 2], KTj[j % 2]
            E = Et[ec[0] % 3]
            ec[0] += 1
            for hf in range(2):
                rows = slice(hf * 64, (hf + 1) * 64)
                for qb in range(2):
                    for kb_ in range(2):
                        cc = c + kb_
                        o = (qb * 2 + kb_) * 128
                        mm(ps_sc[hf][:, o:o + 128], KT_[rows, cc * 128:(cc + 1) * 128],
                           QT_[rows, qb, c * 128:(c + 1) * 128], True, True, [KT_, QT_], [ps_sc[hf]])
            for hf in range(2):
                A(E[:, hf, :, :, :], ps_sc[hf][:, :].rearrange("p (b k q) -> p b k q", k=2, q=128), AF.Exp,
                  [ps_sc[hf]], [E], scale=SCALE)
            kb.op(pool, "affine_select", r=[E], w=[E], out=E[:, :, :, 0, :], in_=E[:, :, :, 0, :],
                  pattern=[[0, 2], [0, 2], [-1, 128]], compare_op=ALU.is_gt, fill=0.0, base=0, channel_multiplier=1)
            kb.op(pool, "affine_select", r=[E], w=[E], out=E[:, :, :, 1, :], in_=E[:, :, :, 1, :],
                  pattern=[[0, 2], [0, 2], [1, 128]], compare_op=ALU.is_ge, fill=0.0, base=0, channel_multiplier=-1)
            return E

        def att_out(j, c, E):
            Vt_ = Vtj[j % 2]
            oh = ones_hv if c <= 1 else ones
            for hf in range(2):
                for qb in range(2):
                    h = 4 * j + 2 * qb + hf
                    o = (hf * 2 + qb) * 128
                    mm(ps_r[:, o:o + 128], oh[:, :], E[:, hf, qb, 0, :], True, False, [oh, E], [ps_r])
                    mm(ps_r[:, o:o + 128], ones[:, :], E[:, hf, qb, 1, :], False, True, [ones, E], [ps_r])
            for hf in range(2):
                for qb in range(2):
                    o = (hf * 2 + qb) * 128
                    mm(ps_o[:, o:o + 128], Vt_[:, c, :], E[:, hf, qb, 0, :], True, False, [Vt_, E], [ps_o])
                    mm(ps_o[:, o:o + 128], Vt_[:, c + 1, :], E[:, hf, qb, 1, :], False, True, [Vt_, E], [ps_o])
            rr = rrt[rc[0] % 2]
            rc[0] += 1
            rrf = rr[:, :, :, :].rearrange("p a b q -> p (a b q)")
            for hf in range(2):
                for qb in range(2):
                    h = 4 * j + 2 * qb + hf
                    o = (hf * 2 + qb) * 128
                    A(rr[:, hf, qb, :], ps_r[:, o:o + 128], AF.Ln, [ps_r, es_t], [rr], bias=es_t[:, h:h + 1])
            A(rrf, rrf, AF.Exp, [rr], [rr], scale=-1.0)
            for hf in range(2):
                rows = slice(hf * 64, (hf + 1) * 64)
                TTo(dve, yca[rows, 2 * j:2 * j + 2, c * 128:(c + 1) * 128],
                    ps_o[rows, hf * 256:(hf + 1) * 256].rearrange("p (b q) -> p b q", q=128),
                    rr[rows, hf, :, :], ALU.mult, [ps_o, rr], [yca])

        kv_proj(0)
        for j in range(nkv):
            pend = []
            if j + 1 < nkv:
                kb.record()
                kv_proj(j + 1)
                pend = kb.stop()
            per = (len(pend) + 17) // 18
            Ecur = att_scores(j, 0)
            for c in range(9):
                pend = kb.replay(pend, per)
                Enx = att_scores(j, c + 1) if c + 1 < 9 else None
                pend = kb.replay(pend, per)
                att_out(j, c, Ecur)
                Ecur = Enx
            pend = kb.replay(pend)
        if stage in ("att", "att1"):
            dump_bf(yca, 16, 1152)
            return

        kb.barrier()
        NH = 1026
        hlo, HLO_END = ar.alloc_at("hlo", [128, 8, NH], F32, XG_OFF)
        ar.top = ATT_TOP0
        hhi = ar.alloc("hhi", [128, 8, NH], F32)
        H_TOP = ar.top
        hpart = lambda ob: (hlo if ob < 8 else hhi)
        hv_ = lambda ob, a, b: hpart(ob)[:, ob % 8, a:b]
        w_out = din("w_out", [4096, D]).rearrange("(k p) f -> p k f", p=128)
        wso = []
        o_ = DG_OFF
        for i_ in range(3):
            t_, o_ = ar.alloc_at(f"wso{i_}", [128, 32, 128], BF16, o_)
            wso.append(t_)
        xres = [ar.alloc(f"xres{i}", [128, 342], F32) for i in range(3)]
        s_xres = [kb.dsem(f"d_xres{i}") for i in range(3)]
        ps_pj = [bank(0, "ps_pj0"), bank(1, "ps_pj1")]
        oc = [0]
        xc = [0]


        def wso_load(ob):
            i = oc[0] % 3
            oc[0] += 1
            kb.dma(pool, wso[i][:, :, :], w_out[:, :, ob * 128:(ob + 1) * 128], s_wsl[i], w=[wso[i]])
            return wso[i]


        nxt = wso_load(0)
        for ob in range(16):
            cur = nxt
            if ob < 15:
                nxt = wso_load(ob + 1)
            for tt in range(3):
                c0 = 126 + tt * 342
                xi = xc[0] % 3
                xc[0] += 1
                kb.dma(sp, xres[xi][:, :], xT[:, ob, MAIN0 + c0:MAIN0 + c0 + 342], s_xres[xi], w=[xres[xi]])
                pj = ps_pj[pcnt[0] % 2]
                pcnt[0] += 1
                for k in range(32):
                    src = ycs if k < 16 else yca
                    mm(pj[:, 0:342], cur[:, k, :], src[:, k % 16, c0:c0 + 342], k == 0, k == 31, [cur, src], [pj])
                TTo(dve, hv_(ob, tt * 342, (tt + 1) * 342), pj[:, 0:342], xres[xi][:, :], ALU.add, [pj, xres[xi]], [hpart(ob)])


        def dump_h():
            dbg = nc.dram_tensor("dbg", [128, 16 * NH], F32, kind="ExternalOutput").ap()
            s_out = kb.dsem("d_out")
            kb.dma(sp, dbg[:, 0:8 * NH], hlo[:, :, :].rearrange("p b n -> p (b n)"), s_out, r=[hlo])
            kb.dma(sp, dbg[:, 8 * NH:16 * NH], hhi[:, :, :].rearrange("p b n -> p (b n)"), s_out, r=[hhi])
            sp.h.wait_ge(s_out.h, s_out.count)


        if stage == "oproj":
            dump_h()
            return


        def norm_from_h(gbase, xg_dst, rs_dst, col0, ncols, sqb, lnb, pss):
            tl = []
            o = 0
            while o < ncols:
                n = min(512, ncols - o)
                tl.append((o, n))
                o += n
            for (o, n) in tl:
                for k in range(16):
                    sqi = sqb[k % 2]
                    A(sqi[:, 0:n], hv_(k, col0 + o, col0 + o + n), AF.Square, [hpart(k)], [sqi])
                    TS(dve, xg_dst[:, k, o:o + n], hv_(k, col0 + o, col0 + o + n), vec[:, gbase + k:gbase + k + 1], ALU.mult,
                       [hpart(k), vec], [xg_dst])
                    mm(pss[:, 0:n], ones[:, :], sqi[:, 0:n], k == 0, k == 15, [ones, sqi], [pss])
                A(lnb[:, 0:n], pss[:, 0:n], AF.Ln, [pss], [lnb], scale=1.0 / D, bias=EPS)
                A(rs_dst[:, o:o + n], lnb[:, 0:n], AF.Exp, [lnb], [rs_dst], scale=-0.5)


        kb.barrier()
        ar.top = MB_TOP0
        w_up = din("w_up", [D, 2 * DFF]).rearrange("(k p) f -> p k f", p=128)
        w_down = din("w_down", [DFF, D]).rearrange("(k p) f -> p k f", p=128)
        fcw = din("fcw", [128, 88, 3])
        fcb = din("fcb", [128, 88])
        xg2 = ar.alloc("xg2", [128, 16, NH], BF16)
        actq = ar.alloc("actq", [128, 11, 1024], BF16)
        rstd2 = ar.alloc("rstd2", [128, NH], F32)
        fw = ar.alloc("fw", [128, 88, 3], F32)
        fb = ar.alloc("fb", [128, 88], F32)
        wdn = [ar.alloc(f"wdn{i}", [128, 11, 256], BF16) for i in range(2)]
        s_wdn = [kb.dsem(f"d_wdn{i}") for i in range(2)]
        s_fw = kb.dsem("d_fw")
        kb.dma(sp, fw[:, :, :], fcw[:, :, :], s_fw, w=[fw])
        kb.dma(sp, fb[:, :], fcb[:, :], s_fw, w=[fb])
        fw.b.w = {s_fw: s_fw.count}
        assert ar.top <= ATT_TOP0, ar.top
        ar.top = H_TOP
        sqb = [ar.alloc(f"sqb{i}", [128, 512], BF16) for i in range(2)]
        lnb = ar.alloc("lnb", [128, 512], F32)
        preF = [ar.alloc(f"preF{i}", [128, NH], BF16) for i in range(4)]
        dgf = [ar.alloc(f"dgf{i}", [128, 3, 128], BF16) for i in range(4)]
        sgb = [ar.alloc(f"sgb{i}", [128, 512], F32) for i in range(2)]
        wup = []
        o_ = DG_OFF
        for i_ in range(6):
            t_, o_ = ar.alloc_at(f"wup{i_}", [128, 16, 128], BF16, o_)
            wup.append(t_)
        s_wup = [kb.dsem(f"d_wup{i}") for i in range(6)]
        ps_ss2 = bank(7, "ps_ss2")
        ps_cvF = [bank(2, "ps_cvF0"), bank(3, "ps_cvF1"), bank(4, "ps_cvF2"), bank(5, "ps_cvF3")]
        norm_from_h(V_GF, xg2, rstd2, 0, NH, sqb, lnb, ps_ss2)
        uc = [0]
        fc = [0]
        sc_ = [0]


        def wup_load(col0):
            i = uc[0] % 6
            uc[0] += 1
            kb.dma(pool, wup[i][:, :, :], w_up[:, :, col0:col0 + 128], s_wup[i], w=[wup[i]])
            return wup[i]


        def ffn_block_cols(jj):
            return jj * 128, DFF + jj * 128


        dn_c = [0]
        pend = [wup_load(ffn_block_cols(0)[0]), wup_load(ffn_block_cols(0)[1])]
        for qd in range(4):
            for j in range(11):
                jj = qd * 11 + j
                cur_g, cur_u = pend
                if jj < 43:
                    pend = [wup_load(ffn_block_cols(jj + 1)[0]), wup_load(ffn_block_cols(jj + 1)[1])]
                pcs = []
                for which, wsl_, fblk in ((0, cur_g, jj), (1, cur_u, 44 + jj)):
                    pf = preF[fc[0] % 4]
                    dgi = dgf[fc[0] % 4]
                    fc[0] += 1
                    for k in range(3):
                        TS(dve, dgi[:, k, :], ident[:, :], fw[:, fblk, k:k + 1], ALU.mult, [ident, fw], [dgi])
                    for tt in range(3):
                        pj = ps_pj[pcnt[0] % 2]
                        pcnt[0] += 1
                        for k in range(16):
                            mm(pj[:, 0:342], wsl_[:, k, :], xg2[:, k, tt * 342:(tt + 1) * 342], k == 0, k == 15, [wsl_, xg2], [pj])
                        TTo(dve, pf[:, tt * 342:(tt + 1) * 342], pj[:, 0:342], rstd2[:, tt * 342:(tt + 1) * 342], ALU.mult,
                            [pj, rstd2], [pf])
                    TS(dve, pf[:, 0:2], pf[:, 0:2], hvc, ALU.mult, [pf, vec], [pf])
                    for t2 in range(2):
                        pc = ps_cvF[which * 2 + t2]
                        for k in range(3):
                            mm(pc[:, :], dgi[:, k, :], pf[:, t2 * 512 + k:t2 * 512 + k + 512], k == 0, k == 2, [dgi, pf], [pc])
                        pcs.append(pc)
                for t2 in range(2):
                    sg = sgb[sc_[0] % 2]
                    sc_[0] += 1
                    A(sg[:, :], pcs[t2][:, :], AF.Silu, [pcs[t2], fb], [sg], bias=fb[:, jj:jj + 1])
                    STT(actq[:, j, t2 * 512:(t2 + 1) * 512], pcs[2 + t2][:, :], fb[:, 44 + jj:44 + jj + 1], sg[:, :], ALU.add,
                        ALU.mult, [pcs[2 + t2], fb, sg], [actq])
            for op_ in range(8):
                i = dn_c[0] % 2
                dn_c[0] += 1
                kb.dma(pool, wdn[i][:, :, :], w_down[:, qd * 11:(qd + 1) * 11, op_ * 256:(op_ + 1) * 256], s_wdn[i], w=[wdn[i]])
                for bb in range(2):
                    ob = op_ * 2 + bb
                    for t2 in range(2):
                        pj = ps_pj[pcnt[0] % 2]
                        pcnt[0] += 1
                        for k in range(11):
                            mm(pj[:, :], wdn[i][:, k, bb * 128:(bb + 1) * 128], actq[:, k, t2 * 512:(t2 + 1) * 512], k == 0, k == 10,
                               [wdn[i], actq], [pj])
                        TTo(dve, hv_(ob, 2 + t2 * 512, 2 + (t2 + 1) * 512), hv_(ob, 2 + t2 * 512, 2 + (t2 + 1) * 512), pj[:, :],
                            ALU.add, [hpart(ob), pj], [hpart(ob)])
        if stage == "ffn":
            dump_h()
            return

        kb.barrier()
        ar.top = MB_TOP0
        w_pg = din("w_pg", [D, D]).rearrange("(k p) f -> p k f", p=128)
        w_pe = din("w_pe", [256, D]).rearrange("(k p) f -> p k f", p=128)
        pTd = din("pT", [256, 1024]).rearrange("(k p) t -> p k t", p=128)
        outd = nc.dram_tensor("out", [D, 1024], F32, kind="ExternalOutput").ap().rearrange("(k p) t -> p k t", p=128)
        xg3 = ar.alloc("xg3", [128, 16, 1024], BF16)
        peT = ar.alloc("peT", [128, 16, 1024], BF16)
        pTb = ar.alloc("pTb", [128, 2, 1024], BF16)
        assert ar.top <= ATT_TOP0, ar.top
        ar.top = H_TOP
        sqb = [ar.alloc(f"sqb{i}", [128, 512], BF16) for i in range(2)]
        lnb = ar.alloc("lnb", [128, 512], F32)
        rpe = ar.alloc("rpe", [128, 1024], F32)
        Wpe, e_ = ar.alloc_at("Wpe", [128, 2, 2048], BF16, HLO_END)
        rstd3, e_ = ar.alloc_at("rstd3", [128, 1024], F32, e_)
        assert e_ <= MB_TOP0, (e_, MB_TOP0)
        sig = [ar.alloc(f"sig{i}", [128, 512], F32) for i in range(2)]
        pet = [ar.alloc(f"pet{i}", [128, 512], F32) for i in range(2)]
        s_pl = kb.dsem("d_pl")
        kb.dma(pool, pTb[:, :, :], pTd[:, :, :], s_pl, w=[pTb])
        kb.dma(pool, Wpe[:, :, :], w_pe[:, :, :], s_pl, w=[Wpe])
        pTb.b.w = {s_pl: s_pl.count}
        wsl = []
        o_ = DG_OFF
        for i_ in range(3):
            t_, o_ = ar.alloc_at(f"wslg{i_}", [128, 16, 256], BF16, o_)
            wsl.append(t_)
        ps_ssp = [bank(2, "ps_ssp0"), bank(3, "ps_ssp1")]
        norm_from_h(V_GP, xg3, rstd3, 2, 1024, sqb, lnb, ps_ss2)
        plag = []
        for ob in range(16):
            for t2 in range(2):
                pj = ps_pj[pcnt[0] % 2]
                pcnt[0] += 1
                for k in range(2):
                    mm(pj[:, :], Wpe[:, k, ob * 128:(ob + 1) * 128], pTb[:, k, t2 * 512:(t2 + 1) * 512], k == 0, k == 1,
                       [Wpe, pTb], [pj])
                A(peT[:, ob, t2 * 512:(t2 + 1) * 512], pj[:, :], AF.Copy, [pj], [peT])
                sqi = sqb[(ob * 2 + t2) % 2]
                A(sqi[:, :], pj[:, :], AF.Square, [pj], [sqi])
                plag.append((t2, sqi, ob))
                if len(plag) > 1:
                    t2_, sq_, ob_ = plag.pop(0)
                    mm(ps_ssp[t2_][:, :], ones[:, :], sq_[:, :], ob_ == 0, ob_ == 15, [ones, sq_], [ps_ssp[t2_]])
        while plag:
            t2_, sq_, ob_ = plag.pop(0)
            mm(ps_ssp[t2_][:, :], ones[:, :], sq_[:, :], ob_ == 0, ob_ == 15, [ones, sq_], [ps_ssp[t2_]])
        for t2 in range(2):
            A(lnb[:, :], ps_ssp[t2][:, :], AF.Ln, [ps_ssp[t2]], [lnb], scale=1.0 / D, bias=EPS)
            A(rpe[:, t2 * 512:(t2 + 1) * 512], lnb[:, :], AF.Exp, [lnb], [rpe], scale=-0.5)
        s_fin = kb.dsem("d_fin")
        gc = [0]
        nxt = wload(w_pg[:, :, 0:256])
        for sl in range(8):
            cur = nxt
            if sl < 7:
                nxt = wload(w_pg[:, :, (sl + 1) * 256:(sl + 2) * 256])
            for bb in range(2):
                ob = sl * 2 + bb
                for t2 in range(2):
                    cs_ = slice(t2 * 512, (t2 + 1) * 512)
                    pj = ps_pj[pcnt[0] % 2]
                    pcnt[0] += 1
                    for k in range(16):
                        mm(pj[:, :], cur[:, k, bb * 128:(bb + 1) * 128], xg3[:, k, cs_], k == 0, k == 15, [cur, xg3], [pj])
                    sg = sig[gc[0] % 2]
                    pt = pet[gc[0] % 2]
                    gc[0] += 1
                    TTo(dve, sg[:, :], pj[:, :], rstd3[:, cs_], ALU.mult, [pj, rstd3], [sg])
                    A(sg[:, :], sg[:, :], AF.Sigmoid, [sg], [sg])
                    STT(pt[:, :], peT[:, ob, cs_], vec[:, V_GPP + ob:V_GPP + ob + 1], rpe[:, cs_], ALU.mult, ALU.mult,
                        [peT, vec, rpe], [pt])
                    TTo(dve, pt[:, :], pt[:, :], sg[:, :], ALU.mult, [pt, sg], [pt])
                    TTo(dve, hv_(ob, 2 + t2 * 512, 2 + (t2 + 1) * 512), hv_(ob, 2 + t2 * 512, 2 + (t2 + 1) * 512), pt[:, :],
                        ALU.add, [hpart(ob), pt], [hpart(ob)])
                kb.dma(sp, outd[:, ob, :], hv_(ob, 2, 1026), s_fin, r=[hpart(ob)])
        sp.h.wait_ge(s_fin.h, s_fin.count)


    try:
        _main_pass()
    except _Cut:
        dbg = nc.dram_tensor("dbg", [128, 64], F32, kind="ExternalOutput").ap()
        s_out = kb.dsem("d_out")
        kb.barrier()
        kb.dma(sp, dbg[:, :], identf[:, 0:64], s_out, r=[identf])
        sp.h.wait_ge(s_out.h, s_out.count)
    es.close()
    return nc


def make_consts():
    c = np.zeros((128, 1024), np.float32)
    j = np.arange(128)
    c[:, 0:128] = np.eye(128, dtype=np.float32)
    c[:, 128:256] = (j[:, None] <= j[None, :]).astype(np.float32)
    c[:, 256:384] = (j[:, None] > j[None, :]).astype(np.float32)
    c[:, 384:512] = 1.0
    Rm = np.zeros((128, 128), np.float32)
    for hf in range(2):
        for i in range(8):
            Rm[64 * hf + i, 64 * hf + i + 8] = -1.0
            Rm[64 * hf + i + 8, 64 * hf + i] = 1.0
    c[:, 512:640] = Rm.T
    bd = np.zeros((128, 128), np.float32)
    bd[0:64, 0:64] = 1.0
    bd[64:128, 64:128] = 1.0
    c[:, 640:768] = bd
    c32 = np.zeros((32, 768), np.float32)
    h = np.arange(32)
    m = np.arange(128)
    c32[:, 0:128] = ((h[:, None] % 2) == (m[None, :] // 64)).astype(np.float32)
    c32[:, 128:144] = ((h[:, None] // 2) == np.arange(16)[None, :]).astype(np.float32)
    rs = np.ones(512, np.float32)
    rs[::128] = 0.0
    c32[:, 256:768] = rs[None, :]
    return c, c32


def col16(v):
    return np.ascontiguousarray(np.asarray(v, np.float32).reshape(-1, 128).T)


def make_in_maps(inputs):
    f = lambda k: np.asarray(inputs[k], np.float32)
    x = f("x")[0]
    xTfull = np.ascontiguousarray(x.T)
    c, c32 = make_consts()
    conv_w = f("conv_w")[0]
    cwl = np.ascontiguousarray(conv_w.reshape(4, 24, 128).transpose(2, 1, 0))
    vec = np.zeros((128, 256), np.float32)
    vec[:, 0:16] = col16(f("attn_norm_g")[0])
    vec[:, 16:40] = col16(f("conv_b")[0])
    d_skip = f("d_skip")[0]
    p = np.arange(128)
    for b in range(16):
        vec[:, 40 + b] = d_skip[2 * b + p // 64]
    vec[:, 56:72] = col16(f("ssm_norm_g")[0])
    vec[:, 72] = f("q_norm_g")[0][p % 64]
    vec[:, 73] = f("k_norm_g")[0][p % 64]
    invf = (500000.0 ** (-np.arange(0, 16, 2, dtype=np.float32) / 16)).astype(np.float32)
    i64 = p % 64
    vec[:, 75] = np.where(i64 < 16, invf[i64 % 8], 0.0)
    vec[:, 76:108] = f("sinks")[0][None, :]
    vec[:, 108:124] = col16(f("ffn_norm_g")[0])
    vec[:, 124:140] = col16(f("ple_norm_g")[0])
    vec[:, 140:156] = col16(f("ple_post_g")[0])
    pos_all = np.asarray(inputs["positions"]).astype(np.int32)[0]
    pfull = f("p")[0, 0]
    fcw_ = f("ffn_conv_w")[0]
    fcwl = np.ascontiguousarray(fcw_.reshape(3, 88, 128).transpose(2, 1, 0))
    fcbl = col16(f("ffn_conv_b")[0])
    maps = []
    for ci in range(NCORES):
        xt = np.zeros((D, NSLOT), np.float32)
        nprev = min(ci * 1024, 7168)
        own0 = ci * 1024
        xt[:, 7168:] = xTfull[:, own0:own0 + 1024]
        if nprev:
            xt[:, 7168 - nprev:7168] = xTfull[:, own0 - nprev:own0]
        mk = np.zeros((32, NSLOT), np.float32)
        mk[:, 7168 - nprev:] = 1.0
        v = vec.copy()
        v[:, 74] = 1.0 if ci > 0 else 0.0
        g0 = own0 - 256
        posv = np.zeros((1280,), np.int32)
        lo = max(0, -g0)
        posv[lo:] = pos_all[g0 + lo:g0 + 1280]
        m = {
            "xT": xt, "msk": mk, "cst": c, "cst32": c32, "vecs": v,
            "w_in": f("w_in")[0],
            "conv_w": cwl,
            "dt_bias": f("dt_bias")[0].reshape(32, 1).copy(),
            "a_log": f("a_log")[0].reshape(32, 1).copy(),
            "pos": np.ascontiguousarray(np.broadcast_to(posv[None, :], (128, 1280))),
            "w_out": f("w_out")[0], "w_up": f("w_up")[0], "w_down": f("w_down")[0],
            "fcw": fcwl, "fcb": fcbl,
            "w_pg": f("w_ple_gate")[0], "w_pe": f("w_ple_proj")[0],
            "pT": np.ascontiguousarray(pfull[own0:own0 + 1024].T),
        }
        maps.append(m)
    return maps


def kernel(**inputs):
    nc = build_nc("full")
    maps = make_in_maps(inputs)
    res = run_bass_kernel_spmd(nc, maps, core_ids=list(range(NCORES)))
    outs = [r["out"] for r in res.results]
    full = np.concatenate([o.T for o in outs], axis=0)[None]
    return np.ascontiguousarray(full.astype(np.float32))
```

```python
import numpy as np
from contextlib import ExitStack
import concourse.bass as bass
import concourse.mybir as mybir
from concourse.bass_utils import run_bass_kernel_spmd

F32, BF16, I32 = mybir.dt.float32, mybir.dt.bfloat16, mybir.dt.int32
AF = mybir.ActivationFunctionType
ALU = mybir.AluOpType

NCORES = 8
D = 2048
SEQ = 8192
TOK = SEQ // NCORES
NSLOT = 8192
NPRE = 55
MAIN0 = NPRE * 128
M10 = MAIN0 - 128
EPS = 1e-6
DFF = 5632
O_Z, O_X, O_B, O_C, O_DT, O_Q, O_K, O_V = 0, 2048, 4096, 4608, 5120, 5152, 7200, 7712
ARENA_BYTES = 212000
LAST_ARENA = None
TWO_PI_HI = 6.28125
TWO_PI_LO = 2.0 * np.pi - 6.28125


class Sem:
    def __init__(self, h):
        self.h = h
        self.count = 0


class Eng:
    def __init__(self, name, h, sem, inorder=False):
        self.name, self.h, self.sem, self.inorder = name, h, sem, inorder
        self.waited = {}


class Buf:
    def __init__(self, name=""):
        self.name = name
        self.w = {}
        self.r = {}
        self.excl = False


def _merge(d, e):
    for k, v in e.items():
        if d.get(k, 0) < v:
            d[k] = v


class T:
    def __init__(self, v, name):
        self.v = v
        self.b = Buf(name)

    def __getitem__(self, k):
        return self.v[k]


class KB:
    def __init__(self, nc, es):
        self.nc, self.es = nc, es
        self.pe = Eng("pe", nc.tensor, self.sem("s_pe"), inorder=True)
        self.act = Eng("act", nc.scalar, self.sem("s_act"))
        self.dve = Eng("dve", nc.vector, self.sem("s_dve"))
        self.pool = Eng("pool", nc.gpsimd, self.sem("s_pool"))
        self.sp = Eng("sp", nc.sync, self.sem("s_sp"))
        self.engs = [self.pe, self.act, self.dve, self.pool, self.sp]
        self.dsems = []
        self.rec = None

    def record(self):
        self.rec = []

    def stop(self):
        r, self.rec = self.rec, None
        return r

    def replay(self, items, n=None):
        n = len(items) if n is None else min(n, len(items))
        for it in items[:n]:
            if it[0] == "op":
                self.op(it[1], it[2], r=it[3], w=it[4], **it[5])
            else:
                self.dma(it[1], it[2], it[3], it[4], r=it[5], w=it[6])
        return items[n:]

    def sem(self, name):
        return Sem(self.es.enter_context(self.nc.semaphore(name)))

    def dsem(self, name):
        s = self.sem(name)
        self.dsems.append(s)
        return s

    def _wait(self, eng, deps):
        for sem, val in deps.items():
            if sem is eng.sem and eng.inorder:
                continue
            if eng.waited.get(sem, 0) < val:
                eng.h.wait_ge(sem.h, val)
                eng.waited[sem] = val

    def _deps(self, r, w):
        deps = {}
        for b in r:
            _merge(deps, b.b.w)
            if b.b.excl:
                _merge(deps, b.b.r)
        for b in w:
            _merge(deps, b.b.w)
            _merge(deps, b.b.r)
        return deps

    def _done(self, ev, r, w):
        for b in w:
            b.b.w = dict(ev)
            b.b.r = {}
        for b in r:
            _merge(b.b.r, ev)

    def op(self, eng, name, r=(), w=(), **kw):
        if self.rec is not None:
            self.rec.append(("op", eng, name, tuple(r), tuple(w), kw))
            return None
        self._wait(eng, self._deps(r, w))
        ins = getattr(eng.h, name)(**kw)
        eng.sem.count += 1
        ins.then_inc(eng.sem.h, 1)
        self._done({eng.sem: eng.sem.count}, r, w)
        return ins

    def dma(self, eng, out, in_, sem, r=(), w=()):
        if self.rec is not None:
            self.rec.append(("dma", eng, out, in_, sem, tuple(r), tuple(w)))
            return None
        self._wait(eng, self._deps(r, w))
        ins = eng.h.dma_start(out=out, in_=in_)
        sem.count += 16
        ins.then_inc(sem.h, 16)
        self._done({sem: sem.count}, r, w)
        return ins

    def barrier(self):
        tot = {}
        for e in self.engs:
            if e.sem.count:
                tot[e.sem] = e.sem.count
        for s in self.dsems:
            if s.count:
                tot[s] = s.count
        for e in self.engs:
            self._wait(e, {k: v for k, v in tot.items() if k is not e.sem})


class Arena:
    def __init__(self, t, nbytes):
        self.t, self.nbytes, self.top = t, nbytes, 0
        self.offs = {}

    def alloc_at(self, name, shape, dt, off):
        save = self.top
        self.top = off
        t = self.alloc(name, shape, dt)
        end = self.top
        self.top = save
        return t, end

    def alloc(self, name, shape, dt):
        P = shape[0]
        free = 1
        for s in shape[1:]:
            free *= s
        esz = 2 if dt == BF16 else 4
        nb = free * esz
        off = self.top
        self.offs[name] = off
        self.top += (nb + 31) // 32 * 32
        self.hw = max(getattr(self, 'hw', 0), self.top)
        assert self.top <= self.nbytes, f"SBUF arena overflow at {name}: {self.top}"
        v = self.t[0:P, off // 2: off // 2 + nb // 2]
        if esz == 4:
            v = v.bitcast(dt)
        if len(shape) > 2:
            names = "abcdef"[:len(shape) - 1]
            kw = {names[i]: shape[i + 1] for i in range(1, len(names))}
            v = v.rearrange(f"p ({' '.join(names)}) -> p {' '.join(names)}", **kw)
        return T(v, name)


class _Cut(Exception):
    pass


def build_nc(stage="full"):
    cutn = None
    if ":cut" in stage:
        stage, c_ = stage.split(":cut")
        cutn = int(c_)

    noprefix = stage.startswith("np_")
    if noprefix:
        stage = stage[3:]

    def cut(n):
        if cutn == n:
            raise _Cut()
    nc = bass.Bass("TRN2", target_bir_lowering=False)
    es = ExitStack()
    kb = KB(nc, es)
    pe, act, dve, pool, sp = kb.pe, kb.act, kb.dve, kb.pool, kb.sp
    global LAST_ARENA
    ar = LAST_ARENA = Arena(es.enter_context(nc.sbuf_tensor("arena", [128, ARENA_BYTES // 2], BF16)), ARENA_BYTES)
    psum = es.enter_context(nc.psum_tensor("psum", [128, 4096], F32))

    def bank(i, name):
        t = T(psum[:, i * 512:(i + 1) * 512], name)
        t.b.excl = True
        return t

    def bank_bf(i, name):
        t = T(psum[:, i * 512:(i + 1) * 512].bitcast(BF16), name)
        t.b.excl = True
        return t

    def din(name, shape, dt=F32):
        return nc.dram_tensor(name, list(shape), dt, kind="ExternalInput").ap()

    def mm(out, lhsT, rhs, start, stop, r, w):
        kb.op(pe, "matmul", r=r, w=w, out=out, lhsT=lhsT, rhs=rhs, start=start, stop=stop)

    def trp(out, in_, idn, r, w):
        kb.op(pe, "transpose", r=r, w=w, out=out, in_=in_, identity=idn)

    def A(out, in_, func, r, w, **kw):
        kb.op(act, "activation", r=r, w=w, out=out, in_=in_, func=func, **kw)

    def TTo(eng, out, in0, in1, op, r, w):
        kb.op(eng, "tensor_tensor", r=r, w=w, out=out, in0=in0, in1=in1, op=op)

    def TS(eng, out, in0, s1, op0, r, w, s2=None, op1=None):
        if op1 is None:
            kb.op(eng, "tensor_scalar", r=r, w=w, out=out, in0=in0, scalar1=s1, scalar2=None, op0=op0)
        else:
            kb.op(eng, "tensor_scalar", r=r, w=w, out=out, in0=in0, scalar1=s1, scalar2=s2, op0=op0, op1=op1)

    def STT(out, in0, scalar, in1, op0, op1, r, w):
        kb.op(dve, "scalar_tensor_tensor", r=r, w=w, out=out, in0=in0, scalar=scalar, in1=in1, op0=op0, op1=op1)

    def CP(eng, out, in_, r, w):
        kb.op(eng, "tensor_copy", r=r, w=w, out=out, in_=in_)

    xT = din("xT", [D, NSLOT]).rearrange("(k p) t -> p k t", p=128)
    msk = din("msk", [32, NSLOT])
    cst = din("cst", [128, 1024])
    cst32 = din("cst32", [32, 768])
    w_in = din("w_in", [D, 8224]).rearrange("(k p) f -> p k f", p=128)
    vecs = din("vecs", [128, 256])
    conv_w = din("conv_w", [128, 24, 4])
    dt_bias = din("dt_bias", [32, 1])
    a_log = din("a_log", [32, 1])

    ident = ar.alloc("ident", [128, 128], BF16)
    identf = ar.alloc("identf", [128, 128], F32)
    Tm = ar.alloc("Tm", [128, 128], BF16)
    Um = ar.alloc("Um", [128, 128], BF16)
    ones = ar.alloc("ones", [128, 128], BF16)
    RmT = ar.alloc("RmT", [128, 128], BF16)
    bdo = ar.alloc("bdo", [128, 128], BF16)
    selH = ar.alloc("selH", [32, 128], F32)
    selB = ar.alloc("selB", [32, 16], F32)
    rstm = ar.alloc("rstm", [32, 512], F32)
    vec = ar.alloc("vec", [128, 256], F32)
    cw = ar.alloc("cw", [128, 24, 4], F32)
    dtb = ar.alloc("dtb", [32, 1], F32)
    acol = ar.alloc("acol", [32, 1], F32)
    ST = ar.alloc("ST", [128, 16, 128], F32)
    V_GA, V_CB, V_DCOL, V_GSSM, V_GQ, V_GK, V_HV, V_INVF = 0, 16, 40, 56, 72, 73, 74, 75
    V_SINK, V_GF, V_GP, V_GPP = 76, 108, 124, 140
    consts_all = [ident, identf, Tm, Um, ones, RmT, bdo, selH, selB, rstm, vec, cw, dtb, acol]

    s_setup = kb.dsem("d_setup")
    s_setup2 = kb.dsem("d_setup2")
    kb.dma(pool, ident[:, :], cst[:, 0:128], s_setup2, w=[ident])
    kb.dma(sp, identf[:, :], cst[:, 0:128], s_setup, w=[identf])
    kb.dma(pool, Tm[:, :], cst[:, 128:256], s_setup2, w=[Tm])
    kb.dma(pool, Um[:, :], cst[:, 256:384], s_setup2, w=[Um])
    kb.dma(pool, ones[:, :], cst[:, 384:512], s_setup2, w=[ones])
    kb.dma(pool, RmT[:, :], cst[:, 512:640], s_setup2, w=[RmT])
    kb.dma(pool, bdo[:, :], cst[:, 640:768], s_setup2, w=[bdo])
    kb.dma(sp, selH[:, :], cst32[:, 0:128], s_setup, w=[selH])
    kb.dma(sp, selB[:, :], cst32[:, 128:144], s_setup, w=[selB])
    kb.dma(sp, rstm[:, :], cst32[:, 256:768], s_setup, w=[rstm])
    kb.dma(sp, vec[:, :], vecs[:, :], s_setup, w=[vec])
    kb.dma(sp, cw[:, :, :], conv_w[:, :, :], s_setup, w=[cw])
    kb.dma(sp, dtb[:, :], dt_bias[:, :], s_setup, w=[dtb])
    kb.dma(sp, acol[:, :], a_log[:, :], s_setup, w=[acol])
    for c_ in consts_all:
        c_.b.w = {s_setup: s_setup.count, s_setup2: s_setup2.count}
    A(acol[:, :], acol[:, :], AF.Exp, [acol], [acol])
    TS(dve, acol[:, :], acol[:, :], -1.0, ALU.mult, [acol], [acol])
    kb.op(dve, "memset", w=[ST], ap=ST[:, :, :], constant=0.0)

    P_TOP = ar.top

    def dt_chain(n, ps_ms, xg_rhs, Wdt, rs, mk, bufs, extra):
        nch = n // 128
        dv, dtm, adt, csT, rem, wT = (bufs[k] for k in ("dv", "dtm", "adt", "csT", "rem", "wT"))
        for k in range(16):
            mm(ps_ms[0:32, 0:n], Wdt[:, k, :], xg_rhs(k), k == 0, k == 15, [Wdt, *extra], [ps_ms])
        TTo(dve, dv[:, 0:n], ps_ms[0:32, 0:n], rs, ALU.mult, [ps_ms, *extra], [dv])
        A(dv[:, 0:n], dv[:, 0:n], AF.Exp, [dv, dtb], [dv], bias=dtb[:, 0:1])
        A(dv[:, 0:n], dv[:, 0:n], AF.Ln, [dv], [dv], bias=1.0)
        TTo(dve, dtm[:, 0:n], dv[:, 0:n], mk[:, 0:n], ALU.mult, [dv, mk], [dtm])
        TS(dve, adt[:, 0:n], dtm[:, 0:n], acol[:, 0:1], ALU.mult, [dtm, acol], [adt])
        kb.op(dve, "tensor_tensor_scan", r=[rstm, adt], w=[csT], out=csT[:, 0:n], data0=rstm[:, 0:n],
              data1=adt[:, 0:n], initial=0.0, op0=ALU.mult, op1=ALU.add)
        cs3 = csT[:, 0:n].rearrange("p (c l) -> p c l", l=128)
        TTo(dve, rem[:, 0:n].rearrange("p (c l) -> p c l", l=128), cs3[:, :, 127:128].broadcast_to([32, nch, 128]),
            cs3, ALU.subtract, [csT], [rem])
        A(rem[:, 0:n], rem[:, 0:n], AF.Exp, [rem], [rem])
        TTo(dve, wT[:, 0:n], rem[:, 0:n], dtm[:, 0:n], ALU.mult, [rem, dtm], [wT])
        if "ecs" in bufs:
            A(bufs["ecs"][:, 0:n], csT[:, 0:n], AF.Exp, [csT], [bufs["ecs"]])

    NT = 512
    DG_OFF = ar.top
    dg = ar.alloc("dg", [128, 96, 128], BF16)
    cbv = lambda blk: vec[:, V_CB + blk:V_CB + blk + 1]
    for blk in range(24):
        for k in range(4):
            TS(dve, dg[:, blk * 4 + k, :], ident[:, :], cw[:, blk, k:k + 1], ALU.mult, [ident, cw], [dg])
    DG_TOP = ar.top
    Wp = ar.alloc("Wp", [128, 16, 2592], BF16)
    hal = ar.alloc("hal", [128, 24, 3], BF16)
    kb.op(dve, "memset", w=[hal], ap=hal[:, :, :], constant=0.0)
    WpS = {}
    for nm_ in ("dt", "B", "x0", "x1", "x2", "x3"):
        t_ = T(Wp.v, "Wp_" + nm_)
        WpS[nm_] = (t_, kb.dsem("d_wp_" + nm_))
    kb.dma(pool, Wp[:, :, 2560:2592], w_in[:, :, O_DT:O_DT + 32], WpS["dt"][1], w=[WpS["dt"][0]])
    kb.dma(pool, Wp[:, :, 2048:2560], w_in[:, :, O_B:O_B + 512], WpS["B"][1], w=[WpS["B"][0]])
    for j in range(4):
        kb.dma(pool, Wp[:, :, j * 512:(j + 1) * 512], w_in[:, :, O_X + j * 512:O_X + (j + 1) * 512], WpS[f"x{j}"][1],
               w=[WpS[f"x{j}"][0]])

    xin = [ar.alloc(f"xin{i}", [128, NT], F32) for i in range(5)]
    s_xin = [kb.dsem(f"d_xin{i}") for i in range(5)]
    sq = [ar.alloc(f"sq{i}", [128, NT], BF16) for i in range(4)]
    xg = [ar.alloc(f"xg{i}", [128, 16, NT], BF16) for i in range(2)]
    rstd = [ar.alloc(f"rstd{i}", [128, NT], F32) for i in range(2)]
    lnv = ar.alloc("lnv", [128, NT], F32)
    mskt = [ar.alloc(f"mskt{i}", [32, NT], F32) for i in range(2)]
    s_msk = [kb.dsem(f"d_msk{i}") for i in range(2)]
    dtb_p = {k: ar.alloc("p_" + k, [32, NT], F32) for k in ("dv", "dtm", "adt", "csT", "rem", "wT")}
    wtok = [ar.alloc(f"wtok{i}", [128, 4, 32], F32) for i in range(2)]
    seltot = ar.alloc("seltot", [32, 4, 16], F32)
    decall = [ar.alloc(f"decall{i}", [128, 4, 16], F32) for i in range(2)]
    pre = [ar.alloc(f"pre{i}", [128, 3 + NT], BF16) for i in range(3)]
    post = [ar.alloc(f"post{i}", [128, NT], BF16) for i in range(3)]
    Btok = [ar.alloc(f"Btok{i}", [128, 4, 512], BF16) for i in range(2)]
    Xs = [ar.alloc(f"Xs{i}", [128, 4, 128], BF16) for i in range(3)]

    ps_ss = bank(0, "ps_ss")
    ps_pj = [bank(1, "ps_pj0"), bank(2, "ps_pj1")]
    ps_cv = [bank(3, "ps_cv0"), bank(4, "ps_cv1")]
    ps_trb = bank_bf(5, "ps_tr")
    ps_tr = T(ps_trb[:, 0:512].rearrange("p (c l) -> p c l", l=128), "ps_tr")
    ps_tr.b = ps_trb.b
    ps_stb = bank(6, "ps_st")
    ps_st = T(ps_stb[:, :].rearrange("p (c l) -> p c l", l=128), "ps_st")
    ps_st.b = ps_stb.b
    ps_ms = bank(7, "ps_ms")

    tiles = []
    t0 = 0
    while t0 < NPRE * 128:
        n = min(NT, NPRE * 128 - t0)
        tiles.append((t0, n))
        t0 += n
    if stage == "prefix_small":
        tiles = tiles[:2]
    if noprefix:
        tiles = []

    cnt = {"xin": 0, "sq": 0, "pre": 0, "post": 0, "pj": 0, "cv": 0, "xs": 0}

    lagq = []
    front_dve = [False]

    def front_k(k, t0, n, xgt):
        xi = cnt["xin"] % 5
        cnt["xin"] += 1
        kb.dma(sp, xin[xi][:, 0:n], xT[:, k, t0:t0 + n], s_xin[xi], w=[xin[xi]])
        sqi = sq[cnt["sq"] % 4]
        cnt["sq"] += 1
        A(sqi[:, 0:n], xin[xi][:, 0:n], AF.Square, [xin[xi]], [sqi])
        if front_dve[0]:
            TS(dve, xgt(k), xin[xi][:, 0:n], vec[:, V_GA + k:V_GA + k + 1], ALU.mult, [xin[xi], vec], [xgt.T])
        else:
            A(xgt(k), xin[xi][:, 0:n], AF.Copy, [xin[xi], vec], [xgt.T], scale=vec[:, V_GA + k:V_GA + k + 1])
        lagq.append((k, n, sqi))
        while lagq and (len(lagq) > 2 or k == 15):
            k_, n_, sq_ = lagq.pop(0)
            mm(ps_ss[:, 0:n_], ones[:, :], sq_[:, 0:n_], k_ == 0, k_ == 15, [ones, sq_], [ps_ss])

    def front_fin(n, rs_out, rsT, scale=1.0 / D):
        A(lnv[:, 0:n], ps_ss[:, 0:n], AF.Ln, [ps_ss], [lnv], scale=scale, bias=EPS)
        A(rs_out, lnv[:, 0:n], AF.Exp, [lnv], [rsT], scale=-0.5)

    class XgView:
        def __init__(self, t, c0=0):
            self.T, self.c0 = t, c0
            self.n = None

        def __call__(self, k):
            return self.T[:, k, self.c0:self.c0 + self.n]

    tailmark = [0]

    def prefix_front(ti):
        t0, n = tiles[ti]
        nch = n // 128
        xgt = XgView(xg[ti % 2])
        xgt.n = n
        rs = rstd[ti % 2]
        mk = mskt[ti % 2]
        kb.dma(sp, mk[:, 0:n], msk[:, t0:t0 + n], s_msk[ti % 2], w=[mk])
        for k in range(16):
            front_k(k, t0, n, xgt)
        front_fin(n, rs[:, 0:n], rs)
        WdtT = T(Wp[:, :, 2560:2592], "Wdt")
        WdtT.b = WpS["dt"][0].b
        dt_chain(n, ps_ms, xgt, WdtT, rs[0:32, 0:n], mk, dtb_p, [xgt.T, rs])
        wT, csT = dtb_p["wT"], dtb_p["csT"]
        wk = wtok[ti % 2]
        if kb.rec is not None:
            tailmark[0] = len(kb.rec)
        for c in range(nch):
            trp(ps_ms[:, c * 32:(c + 1) * 32], wT[:, c * 128:(c + 1) * 128], identf[0:32, 0:32], [wT, identf], [ps_ms])
        CP(dve, wk[:, 0:nch, :], ps_ms[:, 0:nch * 32].rearrange("p (c h) -> p c h", h=32), [ps_ms], [wk])
        dk = decall[ti % 2]
        cs3 = csT[:, 0:n].rearrange("p (c l) -> p c l", l=128)
        TTo(dve, seltot[:, 0:nch, :], selB[:, :].unsqueeze(1).broadcast_to([32, nch, 16]),
            cs3[:, :, 127:128].broadcast_to([32, nch, 16]), ALU.mult, [selB, csT], [seltot])
        mm(ps_ms[:, 128:128 + nch * 16], selH[:, :], seltot[:, 0:nch, :].rearrange("p c b -> p (c b)"), True, True,
           [selH, seltot], [ps_ms])
        A(dk[:, 0:nch, :], ps_ms[:, 128:128 + nch * 16].rearrange("p (c b) -> p c b", b=16), AF.Exp, [ps_ms], [dk])


    pend = []
    if tiles:
        prefix_front(0)
    for ti, (t0, n) in enumerate(tiles):
        nch = n // 128
        xgt = XgView(xg[ti % 2])
        xgt.n = n
        rs = rstd[ti % 2]
        wk = wtok[ti % 2]
        dk = decall[ti % 2]
        if ti + 1 < len(tiles):
            kb.record()
            prefix_front(ti + 1)
            pend = kb.stop()
        nhead = tailmark[0] if pend else 0
        sched = [0] * 24
        for bi_ in range(14):
            sched[bi_] = (nhead * (bi_ + 1)) // 14 - (nhead * bi_) // 14
        ntail = len(pend) - nhead
        for bi_ in range(4):
            sched[16 + bi_] = (ntail * (bi_ + 1)) // 4 - (ntail * bi_) // 4
        bt = Btok[ti % 2]
        ctx = {}

        def S1(bi):
            isB = bi < 4
            col0 = (2048 + bi * 128) if isB else (bi - 4) * 128
            cblk = (16 + bi) if isB else (bi - 4)
            pj = ps_pj[cnt["pj"] % 2]
            cnt["pj"] += 1
            wslab_ = WpS["B" if isB else f"x{(bi - 4) // 4}"][0]
            for k in range(16):
                mm(pj[:, 0:n], Wp[:, k, col0:col0 + 128], xgt(k), k == 0, k == 15, [wslab_, xgt.T], [pj])
            pr = pre[cnt["pre"] % 3]
            cnt["pre"] += 1
            TTo(dve, pr[:, 3:3 + n], pj[:, 0:n], rs[:, 0:n], ALU.mult, [pj, rs], [pr])
            CP(dve, pr[:, 0:3], hal[:, cblk, :], [hal], [pr])
            CP(dve, hal[:, cblk, :], pr[:, n:n + 3], [pr], [hal])
            ctx[bi] = {"pr": pr, "cblk": cblk}

        def S2(bi):
            c_ = ctx[bi]
            pr, cblk = c_["pr"], c_["cblk"]
            pc = ps_cv[cnt["cv"] % 2]
            cnt["cv"] += 1
            for k in range(4):
                mm(pc[:, 0:n], dg[:, cblk * 4 + k, :], pr[:, k:k + n], k == 0, k == 3, [dg, pr], [pc])
            po = post[cnt["post"] % 3]
            cnt["post"] += 1
            A(po[:, 0:n], pc[:, 0:n], AF.Silu, [pc, vec], [po], bias=cbv(cblk))
            c_["po"] = po

        def S3(bi):
            c_ = ctx[bi]
            po = c_["po"]
            for c in range(nch):
                trp(ps_tr[:, c, :], po[:, c * 128:(c + 1) * 128], ident[:, :], [po, ident], [ps_tr])
            if bi < 4:
                A(bt[:, 0:nch, bi * 128:(bi + 1) * 128], ps_tr[:, 0:nch, :], AF.Copy, [ps_tr], [bt])
            else:
                b = bi - 4
                xs = Xs[cnt["xs"] % 3]
                cnt["xs"] += 1
                TTo(dve, xs[:, 0:nch, :].rearrange("p c (h d) -> p c h d", d=64),
                    ps_tr[:, 0:nch, :].rearrange("p c (h d) -> p c h d", d=64),
                    wk[:, 0:nch, 2 * b:2 * b + 2].unsqueeze(3).broadcast_to([128, nch, 2, 64]), ALU.mult,
                    [ps_tr, wk], [xs])
                c_["xs"] = xs

        def S4(bi):
            if bi < 4:
                return
            b = bi - 4
            xs = ctx[bi]["xs"]
            g = b // 4
            for c in range(nch):
                mm(ps_st[:, c, :], xs[:, c, :], bt[:, c, g * 128:(g + 1) * 128], True, True, [xs, bt], [ps_st])
            for c in range(nch):
                STT(ST[:, b, :], ST[:, b, :], dk[:, c, b:b + 1], ps_st[:, c, :], ALU.mult, ALU.add,
                    [ST, dk, ps_st], [ST])

        for i in range(20 + 3):
            if i < 20:
                S1(i)
            pend = kb.replay(pend, sched[i])
            if 0 <= i - 1 < 20:
                S2(i - 1)
            if 0 <= i - 2 < 20:
                S3(i - 2)
            if 0 <= i - 3 < 20:
                S4(i - 3)
        pend = kb.replay(pend)

    if stage.startswith("prefix"):
        dbg = nc.dram_tensor("dbg", [128, 16 * 128], F32, kind="ExternalOutput").ap()
        s_out = kb.dsem("d_out")
        kb.dma(sp, dbg[:, :], ST[:, :, :].rearrange("p b n -> p (b n)"), s_out, r=[ST])
        sp.h.wait_ge(s_out.h, s_out.count)
        es.close()
        return nc

    def _main_pass():
        nonlocal xin, sq, lnv, ps_ss, ps_pj, ps_cv, ps_trb, ps_tr, ps_ms, cnt
        kb.barrier()
        ar.top = DG_TOP
        xg10 = ar.alloc("xg10", [128, 16, 1280], BF16)
        rstd10 = ar.alloc("rstd10", [128, 1280], F32)
        MA_TOP = ar.top
        xin = [ar.alloc(f"xin{i}", [128, NT], F32) for i in range(5)]
        sq = [ar.alloc(f"sq{i}", [128, NT], BF16) for i in range(4)]
        lnv = ar.alloc("lnv", [128, NT], F32)
        front_dve[0] = True
        for (c0, n) in ((0, 512), (512, 512), (1024, 256)):
            xgt = XgView(xg10, c0)
            xgt.n = n
            for k in range(16):
                front_k(k, M10 + c0, n, xgt)
            front_fin(n, rstd10[:, c0:c0 + n], rstd10)
        kb.barrier()
        ar.top = MA_TOP
        cut(1)

        ycs = ar.alloc("ycs", [128, 16, 1152], BF16)
        MB_TOP = ar.top
        NM = 1152
        tk = ar.alloc("tk", [128, 9, 4, 32], F32)
        decb = ar.alloc("decb", [128, 9, 32], F32)
        MB2_TOP = ar.top
        onesf = ar.alloc("onesf", [128, 128], F32)
        Wdt = ar.alloc("Wdt", [128, 16, 32], BF16)
        s_wdt = kb.dsem("d_wdt")
        kb.dma(pool, Wdt[:, :, :], w_in[:, :, O_DT:O_DT + 32], s_wdt, w=[Wdt])
        msk9 = ar.alloc("msk9", [32, 1152], F32)
        s_m9 = kb.dsem("d_m9")
        kb.dma(sp, msk9[:, :], msk[:, MAIN0:MAIN0 + 1152], s_m9, w=[msk9])
        dtm_b = {k: ar.alloc("m_" + k, [32, 384], F32) for k in ("dv", "dtm", "adt", "csT", "rem", "wT", "ecs")}
        CP(dve, onesf[:, :], ones[:, :], [ones], [onesf])
        ps_ms = bank(7, "ps_ms")
        for tt in range(3):
            c0 = 128 + tt * 384
            xgt = XgView(xg10, c0)
            xgt.n = 384
            mkv = T(msk9[:, tt * 384:(tt + 1) * 384], "mkv")
            mkv.b = msk9.b
            dt_chain(384, ps_ms, xgt, Wdt, rstd10[0:32, c0:c0 + 384], mkv, dtm_b, [xg10, rstd10])
            for c in range(3):
                for qi, nm in enumerate(("dtm", "adt", "wT", "ecs")):
                    trp(ps_ms[:, (c * 4 + qi) * 32:(c * 4 + qi + 1) * 32], dtm_b[nm][:, c * 128:(c + 1) * 128],
                        identf[0:32, 0:32], [dtm_b[nm], identf], [ps_ms])
            CP(dve, tk[:, tt * 3:(tt + 1) * 3, :, :],
               ps_ms[:, 0:384].rearrange("p (c q h) -> p c q h", q=4, h=32), [ps_ms], [tk])
            for c in range(3):
                mm(ps_ms[:, 384 + c * 32:384 + (c + 1) * 32], onesf[:, :], tk[:, tt * 3 + c, 1, :], True, True,
                   [onesf, tk], [ps_ms])
            A(decb[:, tt * 3:(tt + 1) * 3, :], ps_ms[:, 384:480].rearrange("p (c h) -> p c h", h=32), AF.Exp,
              [ps_ms], [decb])

        cut(2)
        kb.barrier()
        ar.top = MB2_TOP
        dgm, o_ = ar.alloc_at("dgm", [128, 24, 128], BF16, DG_OFF)
        zt = []
        szb = []
        gsq = []
        for i_ in range(2):
            t_, o_ = ar.alloc_at(f"zt{i_}", [128, 384], F32, o_)
            zt.append(t_)
            t_, o_ = ar.alloc_at(f"szb{i_}", [128, 384], F32, o_)
            szb.append(t_)
            t_, o_ = ar.alloc_at(f"gsq{i_}", [128, 384], BF16, o_)
            gsq.append(t_)
        gacc, o_ = ar.alloc_at("gacc", [128, 1152], F32, o_)
        assert o_ <= DG_OFF + 24576, o_
        wsl = [ar.alloc(f"wsl{i}", [128, 16, 256], BF16) for i in range(3)]
        s_wsl = [kb.dsem(f"d_wsl{i}") for i in range(3)]
        wcnt = [0]

        def wload(src):
            i = wcnt[0] % 3
            wcnt[0] += 1
            kb.dma(pool, wsl[i][:, :, 0:src.shape[2]], src, s_wsl[i], w=[wsl[i]])
            return wsl[i]

        pre9 = [ar.alloc(f"pre9_{i}", [128, 1155], BF16) for i in range(2)]
        XTg = ar.alloc("XTg", [128, 4, NM], BF16)
        BTg = ar.alloc("BTg", [128, NM], BF16)
        CTg = ar.alloc("CTg", [128, NM], BF16)
        Btk = ar.alloc("Btk", [128, 9, 128], BF16)
        Xtk = ar.alloc("Xtk", [128, 9, 512], BF16)
        ytk = ar.alloc("ytk", [128, 9, 512], BF16)
        Sg = ar.alloc("Sg", [128, 512], F32)
        Sbf = ar.alloc("Sbf", [128, 512], BF16)
        CBm2 = [ar.alloc(f"CBm{i}", [128, 128], BF16) for i in range(2)]
        Rb2 = [ar.alloc(f"Rb{i}", [128, 8, 128], BF16) for i in range(2)]
        Eb = [ar.alloc(f"Eb{i}", [128, 4, 128], BF16) for i in range(2)]
        MTb = [ar.alloc(f"MTb{i}", [128, 4, 128], BF16) for i in range(2)]
        Xdt2 = [ar.alloc(f"Xdt{i}", [128, 512], BF16) for i in range(2)]
        Xw = ar.alloc("Xw", [128, 512], BF16)
        t1 = ar.alloc("t1", [128, 512], F32)
        ps_pj = [bank(0, "ps_pj0"), bank(1, "ps_pj1")]
        ps_cv = bank(2, "ps_cv")
        ps_trb = bank_bf(3, "ps_tr")
        ps_seg = bank(4, "ps_seg")
        ps_y2 = [bank(5, "ps_y0"), bank(7, "ps_y1")]
        ps_yo = bank(6, "ps_yo")
        ps_cv2 = [ps_cv, ps_seg]
        ps_trb2 = T(psum[:, 5 * 512:6 * 512].bitcast(BF16), "ps_trb2")
        ps_trb2.b = ps_y2[0].b
        trs = []
        for t_ in (ps_trb, ps_trb2):
            v_ = T(t_[:, 0:512].rearrange("p (c l) -> p c l", l=128), "ps_trx")
            v_.b = t_.b
            trs.append(v_)
        trc = [0]

        def next_tr():
            trc[0] += 1
            return trs[trc[0] % 2]
        pcnt = [0]

        def proj_block_main(wslab, wc0, dst_pre):
            for tt in range(3):
                c0 = 125 + tt * 385
                pj = ps_pj[pcnt[0] % 2]
                pcnt[0] += 1
                for k in range(16):
                    mm(pj[:, 0:385], wslab[:, k, wc0:wc0 + 128], xg10[:, k, c0:c0 + 385], k == 0, k == 15,
                       [wslab, xg10], [pj])
                TTo(dve, dst_pre[:, tt * 385:(tt + 1) * 385], pj[:, 0:385], rstd10[:, c0:c0 + 385], ALU.mult,
                    [pj, rstd10], [dst_pre])

        cvc = [0]

        def conv_main(cblk, li, src_pre, dstT, dst_ap):
            for tt in range(3):
                pcv = ps_cv2[cvc[0] % 2]
                cvc[0] += 1
                for k in range(4):
                    mm(pcv[:, 0:384], dgm[:, li * 4 + k, :], src_pre[:, tt * 384 + k:tt * 384 + k + 384], k == 0, k == 3,
                       [dgm, src_pre], [pcv])
                A(dst_ap(tt), pcv[:, 0:384], AF.Silu, [pcv, vec], [dstT], bias=cbv(cblk))

        ps_sq = []
        for t_ in (ps_cv, ps_trb):
            v_ = T(psum[:, (2 if t_ is ps_cv else 3) * 512:(3 if t_ is ps_cv else 4) * 512], "ps_sq")
            v_.b = t_.b
            ps_sq.append(v_)
        zc = [0]
        w_in_z = lambda j_: w_in[:, :, O_Z + j_ * 256:O_Z + (j_ + 1) * 256]

        def zgate_group(g):
            glag = []

            def gn_step(tt_, qq_, first):
                pq = ps_sq[zc[0] % 2]
                cols_ = slice(tt_ * 384, (tt_ + 1) * 384)
                mm(pq[:, 0:384], ones[:, :], qq_[:, :], True, True, [ones, qq_], [pq])
                if first:
                    CP(dve, gacc[:, cols_], pq[:, 0:384], [pq], [gacc])
                else:
                    TTo(dve, gacc[:, cols_], gacc[:, cols_], pq[:, 0:384], ALU.add, [gacc, pq], [gacc])

            for sl in range(2):
                cur = wload(w_in_z(2 * g + sl))
                for bb in range(2):
                    blk = 4 * g + 2 * sl + bb
                    for tt in range(3):
                        c0 = 128 + tt * 384
                        cols = slice(tt * 384, (tt + 1) * 384)
                        pj = ps_pj[pcnt[0] % 2]
                        pcnt[0] += 1
                        for k in range(16):
                            mm(pj[:, 0:384], cur[:, k, bb * 128:(bb + 1) * 128], xg10[:, k, c0:c0 + 384], k == 0, k == 15,
                               [cur, xg10], [pj])
                        z_ = zt[zc[0] % 2]
                        s_ = szb[zc[0] % 2]
                        q_ = gsq[zc[0] % 2]
                        zc[0] += 1
                        TTo(dve, z_[:, :], pj[:, 0:384], rstd10[:, c0:c0 + 384], ALU.mult, [pj, rstd10], [z_])
                        A(s_[:, :], z_[:, :], AF.Silu, [z_], [s_])
                        TTo(dve, ycs[:, blk, cols], ycs[:, blk, cols], s_[:, :], ALU.mult, [ycs, s_], [ycs])
                        A(q_[:, :], ycs[:, blk, cols], AF.Square, [ycs], [q_])
                        glag.append((tt, q_, blk % 4 == 0))
                        if len(glag) > 1:
                            gn_step(*glag.pop(0))
            while glag:
                gn_step(*glag.pop(0))
            A(gacc[:, :], gacc[:, :], AF.Ln, [gacc], [gacc], scale=1.0 / 512, bias=EPS)
            A(gacc[:, :], gacc[:, :], AF.Exp, [gacc], [gacc], scale=-0.5)
            for b4 in range(4):
                bk = g * 4 + b4
                STT(ycs[:, bk, :], ycs[:, bk, :], vec[:, V_GSSM + bk:V_GSSM + bk + 1], gacc[:, :], ALU.mult, ALU.mult,
                    [ycs, vec, gacc], [ycs])

        ngroups = 1 if stage == "ssd_g0" else 4
        for g in range(4):
            if g >= ngroups:
                break
            slab_bc = wload(w_in[:, :, O_B + g * 128:O_B + (g + 1) * 128])
            slab_c = wload(w_in[:, :, O_C + g * 128:O_C + (g + 1) * 128])
            slab_x = [wload(w_in[:, :, O_X + g * 512:O_X + g * 512 + 256])]
            jobs = [(slab_bc, 0, 16 + g, BTg, lambda tt: BTg[:, tt * 384:(tt + 1) * 384]),
                    (slab_c, 0, 20 + g, CTg, lambda tt: CTg[:, tt * 384:(tt + 1) * 384])]
            for i in range(4):
                jobs.append((None, (i % 2) * 128, 4 * g + i, XTg, lambda tt, i=i: XTg[:, i, tt * 384:(tt + 1) * 384]))
            for li, (_s, _w, cblk_, _d, _a) in enumerate(jobs):
                for k in range(4):
                    TS(dve, dgm[:, li * 4 + k, :], ident[:, :], cw[:, cblk_, k:k + 1], ALU.mult, [ident, cw], [dgm])
            prev = None
            for ji, (slb, wc0, cblk, dstT, dst_ap) in enumerate(jobs):
                if ji == 2:
                    slab_x.append(wload(w_in[:, :, O_X + g * 512 + 256:O_X + g * 512 + 512]))
                if slb is None:
                    slb = slab_x[(ji - 2) // 2]
                pr = pre9[ji % 2]
                proj_block_main(slb, wc0, pr)
                if prev is not None:
                    conv_main(*prev)
                prev = (cblk, ji, pr, dstT, dst_ap)
            conv_main(*prev)
            cut(3)
            for c3 in range(3):
                ps_tr = next_tr()
                for c in range(3):
                    trp(ps_tr[:, c, :], BTg[:, (c3 * 3 + c) * 128:(c3 * 3 + c + 1) * 128], ident[:, :], [BTg, ident], [ps_tr])
                A(Btk[:, c3 * 3:c3 * 3 + 3, :], ps_tr[:, 0:3, :], AF.Copy, [ps_tr], [Btk])
            for i in range(4):
                for c3 in range(3):
                    ps_tr = next_tr()
                    for c in range(3):
                        trp(ps_tr[:, c, :], XTg[:, i, (c3 * 3 + c) * 128:(c3 * 3 + c + 1) * 128], ident[:, :],
                            [XTg, ident], [ps_tr])
                    if trc[0] % 2:
                        A(Xtk[:, c3 * 3:c3 * 3 + 3, i * 128:(i + 1) * 128], ps_tr[:, 0:3, :], AF.Copy, [ps_tr], [Xtk])
                    else:
                        CP(dve, Xtk[:, c3 * 3:c3 * 3 + 3, i * 128:(i + 1) * 128], ps_tr[:, 0:3, :], [ps_tr], [Xtk])
            cut(4)
            for i in range(4):
                mm(ps_yo[:, i * 128:(i + 1) * 128], ST[:, 4 * g + i, :], identf[:, :], True, True, [ST, identf], [ps_yo])
            CP(dve, Sg[:, :], ps_yo[:, :], [ps_yo], [Sg])
            A(Sbf[:, :], ps_yo[:, :], AF.Copy, [ps_yo], [Sbf])
            cut(5)
            hs = slice(g * 8, g * 8 + 8)

            def chunkA(c):
                cs_ = slice(c * 128, (c + 1) * 128)
                cbm, rb, xdt, psy = CBm2[c % 2], Rb2[c % 2], Xdt2[c % 2], ps_y2[c % 2]
                mm(ps_seg[:, 0:128], BTg[:, cs_], CTg[:, cs_], True, True, [BTg, CTg], [ps_seg])
                TTo(dve, cbm[:, :], ps_seg[:, 0:128], Tm[:, :], ALU.mult, [ps_seg, Tm], [cbm])
                TTo(pool, rb[:, :, :], tk[:, c, 1, hs].unsqueeze(2).broadcast_to([128, 8, 128]),
                    Tm[:, :].unsqueeze(1).broadcast_to([128, 8, 128]), ALU.mult, [tk, Tm], [rb])
                TTo(pool, xdt[:, :].rearrange("p (h d) -> p h d", d=64), Xtk[:, c, :].rearrange("p (h d) -> p h d", d=64),
                    tk[:, c, 0, hs].unsqueeze(2).broadcast_to([128, 8, 64]), ALU.mult, [Xtk, tk], [xdt])
                for hf in range(2):
                    mm(ps_seg[:, :], Um[:, :], rb[:, hf * 4:(hf + 1) * 4, :].rearrange("p h l -> p (h l)"), True, True,
                       [Um, rb], [ps_seg])
                    E = Eb[hf]
                    A(E[:, :, :], ps_seg[:, :].rearrange("p (h l) -> p h l", l=128), AF.Exp, [ps_seg], [E])
                    MT = MTb[hf]
                    TTo(dve, MT[:, :, :], E[:, :, :], cbm[:, :].unsqueeze(1).broadcast_to([128, 4, 128]), ALU.mult,
                        [E, cbm], [MT])
                    for hh in range(4):
                        o = (hf * 4 + hh) * 64
                        mm(psy[:, o:o + 64], MT[:, hh, :], xdt[:, o:o + 64], True, True, [MT, xdt], [psy])

            def chunkB(c):
                cs_ = slice(c * 128, (c + 1) * 128)
                psy = ps_y2[c % 2]
                mm(ps_yo[:, :], CTg[:, cs_], Sbf[:, :], True, True, [CTg, Sbf], [ps_yo])
                TTo(dve, t1[:, :].rearrange("p (h d) -> p h d", d=64), ps_yo[:, :].rearrange("p (h d) -> p h d", d=64),
                    tk[:, c, 3, hs].unsqueeze(2).broadcast_to([128, 8, 64]), ALU.mult, [ps_yo, tk], [t1])
                TTo(dve, ytk[:, c, :], psy[:, :], t1[:, :], ALU.add, [psy, t1], [ytk])
                if c < 8:
                    TTo(pool, Xw[:, :].rearrange("p (h d) -> p h d", d=64), Xtk[:, c, :].rearrange("p (h d) -> p h d", d=64),
                        tk[:, c, 2, hs].unsqueeze(2).broadcast_to([128, 8, 64]), ALU.mult, [Xtk, tk], [Xw])
                    mm(ps_yo[:, :], Btk[:, c, :], Xw[:, :], True, True, [Btk, Xw], [ps_yo])
                    TTo(dve, Sg[:, :].rearrange("p (h d) -> p h d", d=64), Sg[:, :].rearrange("p (h d) -> p h d", d=64),
                        decb[:, c, hs].unsqueeze(2).broadcast_to([128, 8, 64]), ALU.mult, [Sg, decb], [Sg])
                    TTo(dve, Sg[:, :], Sg[:, :], ps_yo[:, :], ALU.add, [Sg, ps_yo], [Sg])
                    A(Sbf[:, :], Sg[:, :], AF.Copy, [Sg], [Sbf])

            zp = []
            if g >= 1 and stage not in ("ssd", "ssd_g0"):
                kb.record()
                zgate_group(g - 1)
                zp = kb.stop()
            zburst = (len(zp) + 2) // 3
            chunkA(0)
            for c in range(9):
                if c + 1 < 9:
                    chunkA(c + 1)
                chunkB(c)
                if c in (1, 4, 7):
                    zp = kb.replay(zp, zburst)
            zp = kb.replay(zp)
            cut(6)
            for i in range(4):
                for c3 in range(3):
                    ps_tr = next_tr()
                    for c in range(3):
                        trp(ps_tr[:, c, :], ytk[:, c3 * 3 + c, i * 128:(i + 1) * 128], ident[:, :], [ytk, ident], [ps_tr])
                    STT(ycs[:, 4 * g + i, c3 * 384:(c3 + 1) * 384], XTg[:, i, c3 * 384:(c3 + 1) * 384],
                        vec[:, V_DCOL + 4 * g + i:V_DCOL + 4 * g + i + 1],
                        ps_tr[:, 0:3, :].rearrange("p c l -> p (c l)"), ALU.mult, ALU.add, [XTg, vec, ps_tr], [ycs])

        def dump_bf(t3, nblk, ncol):
            dbg = nc.dram_tensor("dbg", [128, nblk * ncol], BF16, kind="ExternalOutput").ap()
            s_out = kb.dsem("d_out")
            kb.dma(sp, dbg[:, :], t3[:, :, :].rearrange("p b n -> p (b n)"), s_out, r=[t3])
            sp.h.wait_ge(s_out.h, s_out.count)

        if stage in ("ssd", "ssd_g0"):
            dump_bf(ycs, 16, 1152)
            return
        zgate_group(3)
        XG_OFF = ar.offs["xg10"]
        if stage == "gate":
            dump_bf(ycs, 16, 1152)
            return

        kb.barrier()
        ar.top = MB_TOP
        wsl = []
        o_ = DG_OFF
        for i_ in range(3):
            t_, o_ = ar.alloc_at(f"wslb{i_}", [128, 16, 256], BF16, o_)
            wsl.append(t_)
        yca = ar.alloc("yca", [128, 16, 1152], BF16)
        ATT_TOP0 = ar.top
        MB_TOP0 = ar.offs["ycs"]
        pos_i = din("pos", [128, 1280], I32)
        SINt = ar.alloc("SINt", [128, 1280], BF16)
        COSt = ar.alloc("COSt", [128, 1280], BF16)
        es_t = ar.alloc("es_t", [128, 32], F32)
        ones_hv = ar.alloc("ones_hv", [128, 128], BF16)
        hvc = vec[:, V_HV:V_HV + 1]
        A(es_t[:, :], vec[:, V_SINK:V_SINK + 32], AF.Exp, [vec], [es_t])
        TS(dve, ones_hv[:, :], ones[:, :], hvc, ALU.mult, [ones, vec], [ones_hv])
        ATT_TOP = ar.top
        posi = ar.alloc("posi", [128, 1280], I32)
        ang = ar.alloc("ang", [128, 1280], F32)
        kf = ar.alloc("kf", [128, 1280], F32)
        ki = ar.alloc("ki", [128, 1280], I32)
        rr_ = ar.alloc("rr_", [128, 1280], F32)
        m_ = ar.alloc("m_", [128, 1280], F32)
        s_pos = kb.dsem("d_pos")
        kb.dma(sp, posi[:, :], pos_i[:, :], s_pos, w=[posi])
        CP(dve, ang[:, :], posi[:, :], [posi], [ang])
        TS(dve, ang[:, :], ang[:, :], vec[:, V_INVF:V_INVF + 1], ALU.mult, [ang, vec], [ang])
        PI = float(np.pi)
        for which, dst in (("sin", SINt), ("cos", COSt)):
            if which == "cos":
                TS(dve, ang[:, :], ang[:, :], PI / 2, ALU.add, [ang], [ang])
            TS(dve, kf[:, :], ang[:, :], 1.0 / (2 * PI), ALU.mult, [ang], [kf])
            CP(dve, ki[:, :], kf[:, :], [kf], [ki])
            CP(dve, kf[:, :], ki[:, :], [ki], [kf])
            STT(rr_[:, :], kf[:, :], -TWO_PI_HI, ang[:, :], ALU.mult, ALU.add, [kf, ang], [rr_])
            STT(rr_[:, :], kf[:, :], -TWO_PI_LO, rr_[:, :], ALU.mult, ALU.add, [kf, rr_], [rr_])
            TS(dve, m_[:, :], rr_[:, :], PI, ALU.is_gt, [rr_], [m_], s2=-2 * PI, op1=ALU.mult)
            TTo(dve, rr_[:, :], rr_[:, :], m_[:, :], ALU.add, [rr_, m_], [rr_])
            TS(dve, m_[:, :], rr_[:, :], -PI, ALU.is_lt, [rr_], [m_], s2=2 * PI, op1=ALU.mult)
            TTo(dve, rr_[:, :], rr_[:, :], m_[:, :], ALU.add, [rr_, m_], [rr_])
            TS(dve, rr_[:, :], rr_[:, :], PI, ALU.min, [rr_], [rr_], s2=-PI, op1=ALU.max)
            A(dst[:, :], rr_[:, :], AF.Sin, [rr_], [dst])
        cut(10)
        kb.barrier()
        ar.top = ATT_TOP
        QTj = [ar.alloc(f"QTj{i}", [128, 2, 1152], BF16) for i in range(2)]
        KTj = [ar.alloc(f"KTj{i}", [128, 1280], BF16) for i in range(2)]
        Vtj = [ar.alloc(f"Vtj{i}", [128, 10, 128], BF16) for i in range(2)]
        tq = [ar.alloc(f"tq{i}", [128, 512], F32) for i in range(2)]
        sqq = [ar.alloc(f"sqq{i}", [128, 512], BF16) for i in range(2)]
        uq = [ar.alloc(f"uq{i}", [128, 512], BF16) for i in range(2)]
        aq = [ar.alloc(f"aq{i}", [128, 512], F32) for i in range(2)]
        bq = ar.alloc("bq", [128, 512], F32)
        rqq = ar.alloc("rqq", [128, 512], F32)
        vtb = ar.alloc("vtb", [128, 512], BF16)
        Et = [ar.alloc(f"Et{i}", [128, 2, 2, 2, 128], BF16) for i in range(3)]
        rrt = [ar.alloc(f"rrt{i}", [128, 2, 2, 128], F32) for i in range(2)]
        ps_pj = [bank(0, "ps_pj0"), bank(1, "ps_pj1")]
        ps_a = bank(2, "ps_a")
        ps_b = bank(7, "ps_b")
        ps_trb = T(psum[:, 2 * 512:3 * 512].bitcast(BF16), "ps_tr")
        ps_trb.b = ps_a.b
        ps_sc = (bank(3, "ps_s0"), bank(5, "ps_s1"))
        ps_r = bank(4, "ps_r")
        ps_o = bank(6, "ps_o")
        qc = [0]


        qlag = []

        def qk_flush(keep=0):
            while len(qlag) > keep:
                qlag.pop(0)()

        def qk_block(wslab, wc0, gcol, tiles_, tok0, dst_ap, dstT):
            for (o, n) in tiles_:
                c0 = tok0 + o
                pj = ps_pj[pcnt[0] % 2]
                pcnt[0] += 1
                for k in range(16):
                    mm(pj[:, 0:n], wslab[:, k, wc0:wc0 + 128], xg10[:, k, c0:c0 + n], k == 0, k == 15, [wslab, xg10], [pj])
                i_ = qc[0] % 2
                qc[0] += 1
                t_, s_, u_, a_ = tq[i_], sqq[i_], uq[i_], aq[i_]
                TTo(dve, t_[:, 0:n], pj[:, 0:n], rstd10[:, c0:c0 + n], ALU.mult, [pj, rstd10], [t_])
                A(s_[:, 0:n], t_[:, 0:n], AF.Square, [t_], [s_])
                TS(dve, u_[:, 0:n], t_[:, 0:n], gcol, ALU.mult, [t_, vec], [u_])

                def tail(o=o, n=n, c0=c0, s_=s_, u_=u_, a_=a_, dst_ap=dst_ap, dstT=dstT):
                    mm(ps_a[:, 0:n], bdo[:, :], s_[:, 0:n], True, True, [bdo, s_], [ps_a])
                    mm(ps_b[:, 0:n], RmT[:, :], u_[:, 0:n], True, True, [RmT, u_], [ps_b])
                    A(rqq[:, 0:n], ps_a[:, 0:n], AF.Ln, [ps_a], [rqq], scale=1.0 / 64, bias=EPS)
                    A(rqq[:, 0:n], rqq[:, 0:n], AF.Exp, [rqq], [rqq], scale=-0.5)
                    TTo(dve, a_[:, 0:n], u_[:, 0:n], COSt[:, c0:c0 + n], ALU.mult, [u_, COSt], [a_])
                    TTo(dve, bq[:, 0:n], ps_b[:, 0:n], SINt[:, c0:c0 + n], ALU.mult, [ps_b, SINt], [bq])
                    TTo(dve, bq[:, 0:n], bq[:, 0:n], a_[:, 0:n], ALU.add, [bq, a_], [bq])
                    TTo(dve, dst_ap(o, n), bq[:, 0:n], rqq[:, 0:n], ALU.mult, [bq, rqq], [dstT])

                qlag.append(tail)
                qk_flush(keep=1)


        T9 = [(0, 384), (384, 384), (768, 384)]
        T10q = [(0, 384), (384, 384), (768, 512)]
        T10 = [(0, 384), (384, 384), (768, 512)]
        SCALE = 0.125
        ec = [0]
        rc = [0]
        nkv = 1 if stage == "att1" else 8
        ps_tr = T(ps_trb[:, 0:512].rearrange("p (c l) -> p c l", l=128), "ps_tr")
        ps_tr.b = ps_trb.b

        def kv_proj(j):
            QT_, KT_, Vt_ = QTj[j % 2], KTj[j % 2], Vtj[j % 2]
            slab_q = wload(w_in[:, :, O_Q + j * 256:O_Q + (j + 1) * 256])
            i_ = wcnt[0] % 3
            wcnt[0] += 1
            slab_kv = wsl[i_]
            kb.dma(pool, slab_kv[:, :, 0:64], w_in[:, :, O_K + j * 64:O_K + (j + 1) * 64], s_wsl[i_], w=[slab_kv])
            kb.dma(pool, slab_kv[:, :, 64:128], w_in[:, :, O_K + j * 64:O_K + (j + 1) * 64], s_wsl[i_], w=[slab_kv])
            kb.dma(pool, slab_kv[:, :, 128:192], w_in[:, :, O_V + j * 64:O_V + (j + 1) * 64], s_wsl[i_], w=[slab_kv])
            kb.dma(pool, slab_kv[:, :, 192:256], w_in[:, :, O_V + j * 64:O_V + (j + 1) * 64], s_wsl[i_], w=[slab_kv])
            for qb in range(2):
                qk_block(slab_q, qb * 128, vec[:, V_GQ:V_GQ + 1], T9, 128,
                         lambda o, n, qb=qb: QT_[:, qb, o:o + n], QT_)
            qk_block(slab_kv, 0, vec[:, V_GK:V_GK + 1], T10q, 0, lambda o, n: KT_[:, o:o + n], KT_)
            first_v = True
            for (o, n) in T10:
                pj = ps_pj[pcnt[0] % 2]
                pcnt[0] += 1
                for k in range(16):
                    mm(pj[:, 0:n], slab_kv[:, k, 128:256], xg10[:, k, o:o + n], k == 0, k == 15, [slab_kv, xg10], [pj])
                TTo(dve, vtb[:, 0:n], pj[:, 0:n], rstd10[:, o:o + n], ALU.mult, [pj, rstd10], [vtb])
                if first_v:
                    qk_flush()
                    first_v = False
                nch = n // 128
                for c in range(nch):
                    trp(ps_tr[:, c, :], vtb[:, c * 128:(c + 1) * 128], ident[:, :], [vtb, ident], [ps_tr])
                A(Vt_[:, o // 128:o // 128 + nch, :], ps_tr[:, 0:nch, :], AF.Copy, [ps_tr], [Vt_])
            TS(dve, Vt_[:, 0:2, :], Vt_[:, 0:2, :], hvc, ALU.mult, [Vt_, vec], [Vt_])

        def att_scores(j, c):
            QT_, KT_ = QTj[j % 2], KTj[j % 2]
            E = Et[ec[0] % 3]
            ec[0] += 1
            for hf in range(2):
                rows = slice(hf * 64, (hf + 1) * 64)
                for qb in range(2):
                    for kb_ in range(2):
                        cc = c + kb_
                        o = (qb * 2 + kb_) * 128
                        mm(ps_sc[hf][:, o:o + 128], KT_[rows, cc * 128:(cc + 1) * 128],
                           QT_[rows, qb, c * 128:(c + 1) * 128], True, True, [KT_, QT_], [ps_sc[hf]])
            for hf in range(2):
                A(E[:, hf, :, :, :], ps_sc[hf][:, :].rearrange("p (b k q) -> p b k q", k=2, q=128), AF.Exp,
                  [ps_sc[hf]], [E], scale=SCALE)
            kb.op(pool, "affine_select", r=[E], w=[E], out=E[:, :, :, 0, :], in_=E[:, :, :, 0, :],
                  pattern=[[0, 2], [0, 2], [-1, 128]], compare_op=ALU.is_gt, fill=0.0, base=0, channel_multiplier=1)
            kb.op(pool, "affine_select", r=[E], w=[E], out=E[:, :, :, 1, :], in_=E[:, :, :, 1, :],
                  pattern=[[0, 2], [0, 2], [1, 128]], compare_op=ALU.is_ge, fill=0.0, base=0, channel_multiplier=-1)
            return E

        def att_out(j, c, E):
            Vt_ = Vtj[j % 2]
            oh = ones_hv if c <= 1 else ones
            for hf in range(2):
                for qb in range(2):
                    h = 4 * j + 2 * qb + hf
                    o = (hf * 2 + qb) * 128
                    mm(ps_r[:, o:o + 128], oh[:, :], E[:, hf, qb, 0, :], True, False, [oh, E], [ps_r])
                    mm(ps_r[:, o:o + 128], ones[:, :], E[:, hf, qb, 1, :], False, True, [ones, E], [ps_r])
            for hf in range(2):
                for qb in range(2):
                    o = (hf * 2 + qb) * 128
                    mm(ps_o[:, o:o + 128], Vt_[:, c, :], E[:, hf, qb, 0, :], True, False, [Vt_, E], [ps_o])
                    mm(ps_o[:, o:o + 128], Vt_[:, c + 1, :], E[:, hf, qb, 1, :], False, True, [Vt_, E], [ps_o])
            rr = rrt[rc[0] % 2]
            rc[0] += 1
            rrf = rr[:, :, :, :].rearrange("p a b q -> p (a b q)")
            for hf in range(2):
                for qb in range(2):
                    h = 4 * j + 2 * qb + hf
                    o = (hf * 2 + qb) * 128
                    A(rr[:, hf, qb, :], ps_r[:, o:o + 128], AF.Ln, [ps_r, es_t], [rr], bias=es_t[:, h:h + 1])
            A(rrf, rrf, AF.Exp, [rr], [rr], scale=-1.0)
            for hf in range(2):
                rows = slice(hf * 64, (hf + 1) * 64)
                TTo(dve, yca[rows, 2 * j:2 * j + 2, c * 128:(c + 1) * 128],
                    ps_o[rows, hf * 256:(hf + 1) * 256].rearrange("p (b q) -> p b q", q=128),
                    rr[rows, hf, :, :], ALU.mult, [ps_o, rr], [yca])

        kv_proj(0)
        for j in range(nkv):
            pend = []
            if j + 1 < nkv:
                kb.record()
                kv_proj(j + 1)
                pend = kb.stop()
            per = (len(pend) + 17) // 18
            Ecur = att_scores(j, 0)
            for c in range(9):
                pend = kb.replay(pend, per)
                Enx = att_scores(j, c + 1) if c + 1 < 9 else None
                pend = kb.replay(pend, per)
                att_out(j, c, Ecur)
                Ecur = Enx
            pend = kb.replay(pend)
        if stage in ("att", "att1"):
            dump_bf(yca, 16, 1152)
            return

        kb.barrier()
        NH = 1026
        hlo, HLO_END = ar.alloc_at("hlo", [128, 8, NH], F32, XG_OFF)
        ar.top = ATT_TOP0
        hhi = ar.alloc("hhi", [128, 8, NH], F32)
        H_TOP = ar.top
        hpart = lambda ob: (hlo if ob < 8 else hhi)
        hv_ = lambda ob, a, b: hpart(ob)[:, ob % 8, a:b]
        w_out = din("w_out", [4096, D]).rearrange("(k p) f -> p k f", p=128)
        wso = []
        o_ = DG_OFF
        for i_ in range(3):
            t_, o_ = ar.alloc_at(f"wso{i_}", [128, 32, 128], BF16, o_)
            wso.append(t_)
        xres = [ar.alloc(f"xres{i}", [128, 342], F32) for i in range(3)]
        s_xres = [kb.dsem(f"d_xres{i}") for i in range(3)]
        ps_pj = [bank(0, "ps_pj0"), bank(1, "ps_pj1")]
        oc = [0]
        xc = [0]


        def wso_load(ob):
            i = oc[0] % 3
            oc[0] += 1
            kb.dma(pool, wso[i][:, :, :], w_out[:, :, ob * 128:(ob + 1) * 128], s_wsl[i], w=[wso[i]])
            return wso[i]


        nxt = wso_load(0)
        for ob in range(16):
            cur = nxt
            if ob < 15:
                nxt = wso_load(ob + 1)
            for tt in range(3):
                c0 = 126 + tt * 342
                xi = xc[0] % 3
                xc[0] += 1
                kb.dma(sp, xres[xi][:, :], xT[:, ob, MAIN0 + c0:MAIN0 + c0 + 342], s_xres[xi], w=[xres[xi]])
                pj = ps_pj[pcnt[0] % 2]
                pcnt[0] += 1
                for k in range(32):
                    src = ycs if k < 16 else yca
                    mm(pj[:, 0:342], cur[:, k, :], src[:, k % 16, c0:c0 + 342], k == 0, k == 31, [cur, src], [pj])
                TTo(dve, hv_(ob, tt * 342, (tt + 1) * 342), pj[:, 0:342], xres[xi][:, :], ALU.add, [pj, xres[xi]], [hpart(ob)])


        def dump_h():
            dbg = nc.dram_tensor("dbg", [128, 16 * NH], F32, kind="ExternalOutput").ap()
            s_out = kb.dsem("d_out")
            kb.dma(sp, dbg[:, 0:8 * NH], hlo[:, :, :].rearrange("p b n -> p (b n)"), s_out, r=[hlo])
            kb.dma(sp, dbg[:, 8 * NH:16 * NH], hhi[:, :, :].rearrange("p b n -> p (b n)"), s_out, r=[hhi])
            sp.h.wait_ge(s_out.h, s_out.count)


        if stage == "oproj":
            dump_h()
            return


        def norm_from_h(gbase, xg_dst, rs_dst, col0, ncols, sqb, lnb, pss):
            tl = []
            o = 0
            while o < ncols:
                n = min(512, ncols - o)
                tl.append((o, n))
                o += n
            for (o, n) in tl:
                for k in range(16):
                    sqi = sqb[k % 2]
                    A(sqi[:, 0:n], hv_(k, col0 + o, col0 + o + n), AF.Square, [hpart(k)], [sqi])
                    TS(dve, xg_dst[:, k, o:o + n], hv_(k, col0 + o, col0 + o + n), vec[:, gbase + k:gbase + k + 1], ALU.mult,
                       [hpart(k), vec], [xg_dst])
                    mm(pss[:, 0:n], ones[:, :], sqi[:, 0:n], k == 0, k == 15, [ones, sqi], [pss])
                A(lnb[:, 0:n], pss[:, 0:n], AF.Ln, [pss], [lnb], scale=1.0 / D, bias=EPS)
                A(rs_dst[:, o:o + n], lnb[:, 0:n], AF.Exp, [lnb], [rs_dst], scale=-0.5)


        kb.barrier()
        ar.top = MB_TOP0
        w_up = din("w_up", [D, 2 * DFF]).rearrange("(k p) f -> p k f", p=128)
        w_down = din("w_down", [DFF, D]).rearrange("(k p) f -> p k f", p=128)
        fcw = din("fcw", [128, 88, 3])
        fcb = din("fcb", [128, 88])
        xg2 = ar.alloc("xg2", [128, 16, NH], BF16)
        actq = ar.alloc("actq", [128, 11, 1024], BF16)
        rstd2 = ar.alloc("rstd2", [128, NH], F32)
        fw = ar.alloc("fw", [128, 88, 3], F32)
        fb = ar.alloc("fb", [128, 88], F32)
        wdn = [ar.alloc(f"wdn{i}", [128, 11, 256], BF16) for i in range(2)]
        s_wdn = [kb.dsem(f"d_wdn{i}") for i in range(2)]
        s_fw = kb.dsem("d_fw")
        kb.dma(sp, fw[:, :, :], fcw[:, :, :], s_fw, w=[fw])
        kb.dma(sp, fb[:, :], fcb[:, :], s_fw, w=[fb])
        fw.b.w = {s_fw: s_fw.count}
        assert ar.top <= ATT_TOP0, ar.top
        ar.top = H_TOP
        sqb = [ar.alloc(f"sqb{i}", [128, 512], BF16) for i in range(2)]
        lnb = ar.alloc("lnb", [128, 512], F32)
        preF = [ar.alloc(f"preF{i}", [128, NH], BF16) for i in range(4)]
        dgf = [ar.alloc(f"dgf{i}", [128, 3, 128], BF16) for i in range(4)]
        sgb = [ar.alloc(f"sgb{i}", [128, 512], F32) for i in range(2)]
        wup = []
        o_ = DG_OFF
        for i_ in range(6):
            t_, o_ = ar.alloc_at(f"wup{i_}", [128, 16, 128], BF16, o_)
            wup.append(t_)
        s_wup = [kb.dsem(f"d_wup{i}") for i in range(6)]
        ps_ss2 = bank(7, "ps_ss2")
        ps_cvF = [bank(2, "ps_cvF0"), bank(3, "ps_cvF1"), bank(4, "ps_cvF2"), bank(5, "ps_cvF3")]
        norm_from_h(V_GF, xg2, rstd2, 0, NH, sqb, lnb, ps_ss2)
        uc = [0]
        fc = [0]
        sc_ = [0]


        def wup_load(col0):
            i = uc[0] % 6
            uc[0] += 1
            kb.dma(pool, wup[i][:, :, :], w_up[:, :, col0:col0 + 128], s_wup[i], w=[wup[i]])
            return wup[i]


        def ffn_block_cols(jj):
            return jj * 128, DFF + jj * 128


        dn_c = [0]
        pend = [wup_load(ffn_block_cols(0)[0]), wup_load(ffn_block_cols(0)[1])]
        for qd in range(4):
            for j in range(11):
                jj = qd * 11 + j
                cur_g, cur_u = pend
                if jj < 43:
                    pend = [wup_load(ffn_block_cols(jj + 1)[0]), wup_load(ffn_block_cols(jj + 1)[1])]
                pcs = []
                for which, wsl_, fblk in ((0, cur_g, jj), (1, cur_u, 44 + jj)):
                    pf = preF[fc[0] % 4]
                    dgi = dgf[fc[0] % 4]
                    fc[0] += 1
                    for k in range(3):
                        TS(dve, dgi[:, k, :], ident[:, :], fw[:, fblk, k:k + 1], ALU.mult, [ident, fw], [dgi])
                    for tt in range(3):
                        pj = ps_pj[pcnt[0] % 2]
                        pcnt[0] += 1
                        for k in range(16):
                            mm(pj[:, 0:342], wsl_[:, k, :], xg2[:, k, tt * 342:(tt + 1) * 342], k == 0, k == 15, [wsl_, xg2], [pj])
                        TTo(dve, pf[:, tt * 342:(tt + 1) * 342], pj[:, 0:342], rstd2[:, tt * 342:(tt + 1) * 342], ALU.mult,
                            [pj, rstd2], [pf])
                    TS(dve, pf[:, 0:2], pf[:, 0:2], hvc, ALU.mult, [pf, vec], [pf])
                    for t2 in range(2):
                        pc = ps_cvF[which * 2 + t2]
                        for k in range(3):
                            mm(pc[:, :], dgi[:, k, :], pf[:, t2 * 512 + k:t2 * 512 + k + 512], k == 0, k == 2, [dgi, pf], [pc])
                        pcs.append(pc)
                for t2 in range(2):
                    sg = sgb[sc_[0] % 2]
                    sc_[0] += 1
                    A(sg[:, :], pcs[t2][:, :], AF.Silu, [pcs[t2], fb], [sg], bias=fb[:, jj:jj + 1])
                    STT(actq[:, j, t2 * 512:(t2 + 1) * 512], pcs[2 + t2][:, :], fb[:, 44 + jj:44 + jj + 1], sg[:, :], ALU.add,
                        ALU.mult, [pcs[2 + t2], fb, sg], [actq])
            for op_ in range(8):
                i = dn_c[0] % 2
                dn_c[0] += 1
                kb.dma(pool, wdn[i][:, :, :], w_down[:, qd * 11:(qd + 1) * 11, op_ * 256:(op_ + 1) * 256], s_wdn[i], w=[wdn[i]])
                for bb in range(2):
                    ob = op_ * 2 + bb
                    for t2 in range(2):
                        pj = ps_pj[pcnt[0] % 2]
                        pcnt[0] += 1
                        for k in range(11):
                            mm(pj[:, :], wdn[i][:, k, bb * 128:(bb + 1) * 128], actq[:, k, t2 * 512:(t2 + 1) * 512], k == 0, k == 10,
                               [wdn[i], actq], [pj])
                        TTo(dve, hv_(ob, 2 + t2 * 512, 2 + (t2 + 1) * 512), hv_(ob, 2 + t2 * 512, 2 + (t2 + 1) * 512), pj[:, :],
                            ALU.add, [hpart(ob), pj], [hpart(ob)])
        if stage == "ffn":
            dump_h()
            return

        kb.barrier()
        ar.top = MB_TOP0
        w_pg = din("w_pg", [D, D]).rearrange("(k p) f -> p k f", p=128)
        w_pe = din("w_pe", [256, D]).rearrange("(k p) f -> p k f", p=128)
        pTd = din("pT", [256, 1024]).rearrange("(k p) t -> p k t", p=128)
        outd = nc.dram_tensor("out", [D, 1024], F32, kind="ExternalOutput").ap().rearrange("(k p) t -> p k t", p=128)
        xg3 = ar.alloc("xg3", [128, 16, 1024], BF16)
        peT = ar.alloc("peT", [128, 16, 1024], BF16)
        pTb = ar.alloc("pTb", [128, 2, 1024], BF16)
        assert ar.top <= ATT_TOP0, ar.top
        ar.top = H_TOP
        sqb = [ar.alloc(f"sqb{i}", [128, 512], BF16) for i in range(2)]
        lnb = ar.alloc("lnb", [128, 512], F32)
        rpe = ar.alloc("rpe", [128, 1024], F32)
        Wpe, e_ = ar.alloc_at("Wpe", [128, 2, 2048], BF16, HLO_END)
        rstd3, e_ = ar.alloc_at("rstd3", [128, 1024], F32, e_)
        assert e_ <= MB_TOP0, (e_, MB_TOP0)
        sig = [ar.alloc(f"sig{i}", [128, 512], F32) for i in range(2)]
        pet = [ar.alloc(f"pet{i}", [128, 512], F32) for i in range(2)]
        s_pl = kb.dsem("d_pl")
        kb.dma(pool, pTb[:, :, :], pTd[:, :, :], s_pl, w=[pTb])
        kb.dma(pool, Wpe[:, :, :], w_pe[:, :, :], s_pl, w=[Wpe])
        pTb.b.w = {s_pl: s_pl.count}
        wsl = []
        o_ = DG_OFF
        for i_ in range(3):
            t_, o_ = ar.alloc_at(f"wslg{i_}", [128, 16, 256], BF16, o_)
            wsl.append(t_)
        ps_ssp = [bank(2, "ps_ssp0"), bank(3, "ps_ssp1")]
        norm_from_h(V_GP, xg3, rstd3, 2, 1024, sqb, lnb, ps_ss2)
        plag = []
        for ob in range(16):
            for t2 in range(2):
                pj = ps_pj[pcnt[0] % 2]
                pcnt[0] += 1
                for k in range(2):
                    mm(pj[:, :], Wpe[:, k, ob * 128:(ob + 1) * 128], pTb[:, k, t2 * 512:(t2 + 1) * 512], k == 0, k == 1,
                       [Wpe, pTb], [pj])
                A(peT[:, ob, t2 * 512:(t2 + 1) * 512], pj[:, :], AF.Copy, [pj], [peT])
                sqi = sqb[(ob * 2 + t2) % 2]
                A(sqi[:, :], pj[:, :], AF.Square, [pj], [sqi])
                plag.append((t2, sqi, ob))
                if len(plag) > 1:
                    t2_, sq_, ob_ = plag.pop(0)
                    mm(ps_ssp[t2_][:, :], ones[:, :], sq_[:, :], ob_ == 0, ob_ == 15, [ones, sq_], [ps_ssp[t2_]])
        while plag:
            t2_, sq_, ob_ = plag.pop(0)
            mm(ps_ssp[t2_][:, :], ones[:, :], sq_[:, :], ob_ == 0, ob_ == 15, [ones, sq_], [ps_ssp[t2_]])
        for t2 in range(2):
            A(lnb[:, :], ps_ssp[t2][:, :], AF.Ln, [ps_ssp[t2]], [lnb], scale=1.0 / D, bias=EPS)
            A(rpe[:, t2 * 512:(t2 + 1) * 512], lnb[:, :], AF.Exp, [lnb], [rpe], scale=-0.5)
        s_fin = kb.dsem("d_fin")
        gc = [0]
        nxt = wload(w_pg[:, :, 0:256])
        for sl in range(8):
            cur = nxt
            if sl < 7:
                nxt = wload(w_pg[:, :, (sl + 1) * 256:(sl + 2) * 256])
            for bb in range(2):
                ob = sl * 2 + bb
                for t2 in range(2):
                    cs_ = slice(t2 * 512, (t2 + 1) * 512)
                    pj = ps_pj[pcnt[0] % 2]
                    pcnt[0] += 1
                    for k in range(16):
                        mm(pj[:, :], cur[:, k, bb * 128:(bb + 1) * 128], xg3[:, k, cs_], k == 0, k == 15, [cur, xg3], [pj])
                    sg = sig[gc[0] % 2]
                    pt = pet[gc[0] % 2]
                    gc[0] += 1
                    TTo(dve, sg[:, :], pj[:, :], rstd3[:, cs_], ALU.mult, [pj, rstd3], [sg])
                    A(sg[:, :], sg[:, :], AF.Sigmoid, [sg], [sg])
                    STT(pt[:, :], peT[:, ob, cs_], vec[:, V_GPP + ob:V_GPP + ob + 1], rpe[:, cs_], ALU.mult, ALU.mult,
                        [peT, vec, rpe], [pt])
                    TTo(dve, pt[:, :], pt[:, :], sg[:, :], ALU.mult, [pt, sg], [pt])
                    TTo(dve, hv_(ob, 2 + t2 * 512, 2 + (t2 + 1) * 512), hv_(ob, 2 + t2 * 512, 2 + (t2 + 1) * 512), pt[:, :],
                        ALU.add, [hpart(ob), pt], [hpart(ob)])
                kb.dma(sp, outd[:, ob, :], hv_(ob, 2, 1026), s_fin, r=[hpart(ob)])
        sp.h.wait_ge(s_fin.h, s_fin.count)


    try:
        _main_pass()
    except _Cut:
        dbg = nc.dram_tensor("dbg", [128, 64], F32, kind="ExternalOutput").ap()
        s_out = kb.dsem("d_out")
        kb.barrier()
        kb.dma(sp, dbg[:, :], identf[:, 0:64], s_out, r=[identf])
        sp.h.wait_ge(s_out.h, s_out.count)
    es.close()
    return nc


def make_consts():
    c = np.zeros((128, 1024), np.float32)
    j = np.arange(128)
    c[:, 0:128] = np.eye(128, dtype=np.float32)
    c[:, 128:256] = (j[:, None] <= j[None, :]).astype(np.float32)
    c[:, 256:384] = (j[:, None] > j[None, :]).astype(np.float32)
    c[:, 384:512] = 1.0
    Rm = np.zeros((128, 128), np.float32)
    for hf in range(2):
        for i in range(8):
            Rm[64 * hf + i, 64 * hf + i + 8] = -1.0
            Rm[64 * hf + i + 8, 64 * hf + i] = 1.0
    c[:, 512:640] = Rm.T
    bd = np.zeros((128, 128), np.float32)
    bd[0:64, 0:64] = 1.0
    bd[64:128, 64:128] = 1.0
    c[:, 640:768] = bd
    c32 = np.zeros((32, 768), np.float32)
    h = np.arange(32)
    m = np.arange(128)
    c32[:, 0:128] = ((h[:, None] % 2) == (m[None, :] // 64)).astype(np.float32)
    c32[:, 128:144] = ((h[:, None] // 2) == np.arange(16)[None, :]).astype(np.float32)
    rs = np.ones(512, np.float32)
    rs[::128] = 0.0
    c32[:, 256:768] = rs[None, :]
    return c, c32


def col16(v):
    return np.ascontiguousarray(np.asarray(v, np.float32).reshape(-1, 128).T)


def make_in_maps(inputs):
    f = lambda k: np.asarray(inputs[k], np.float32)
    x = f("x")[0]
    xTfull = np.ascontiguousarray(x.T)
    c, c32 = make_consts()
    conv_w = f("conv_w")[0]
    cwl = np.ascontiguousarray(conv_w.reshape(4, 24, 128).transpose(2, 1, 0))
    vec = np.zeros((128, 256), np.float32)
    vec[:, 0:16] = col16(f("attn_norm_g")[0])
    vec[:, 16:40] = col16(f("conv_b")[0])
    d_skip = f("d_skip")[0]
    p = np.arange(128)
    for b in range(16):
        vec[:, 40 + b] = d_skip[2 * b + p // 64]
    vec[:, 56:72] = col16(f("ssm_norm_g")[0])
    vec[:, 72] = f("q_norm_g")[0][p % 64]
    vec[:, 73] = f("k_norm_g")[0][p % 64]
    invf = (500000.0 ** (-np.arange(0, 16, 2, dtype=np.float32) / 16)).astype(np.float32)
    i64 = p % 64
    vec[:, 75] = np.where(i64 < 16, invf[i64 % 8], 0.0)
    vec[:, 76:108] = f("sinks")[0][None, :]
    vec[:, 108:124] = col16(f("ffn_norm_g")[0])
    vec[:, 124:140] = col16(f("ple_norm_g")[0])
    vec[:, 140:156] = col16(f("ple_post_g")[0])
    pos_all = np.asarray(inputs["positions"]).astype(np.int32)[0]
    pfull = f("p")[0, 0]
    fcw_ = f("ffn_conv_w")[0]
    fcwl = np.ascontiguousarray(fcw_.reshape(3, 88, 128).transpose(2, 1, 0))
    fcbl = col16(f("ffn_conv_b")[0])
    maps = []
    for ci in range(NCORES):
        xt = np.zeros((D, NSLOT), np.float32)
        nprev = min(ci * 1024, 7168)
        own0 = ci * 1024
        xt[:, 7168:] = xTfull[:, own0:own0 + 1024]
        if nprev:
            xt[:, 7168 - nprev:7168] = xTfull[:, own0 - nprev:own0]
        mk = np.zeros((32, NSLOT), np.float32)
        mk[:, 7168 - nprev:] = 1.0
        v = vec.copy()
        v[:, 74] = 1.0 if ci > 0 else 0.0
        g0 = own0 - 256
        posv = np.zeros((1280,), np.int32)
        lo = max(0, -g0)
        posv[lo:] = pos_all[g0 + lo:g0 + 1280]
        m = {
            "xT": xt, "msk": mk, "cst": c, "cst32": c32, "vecs": v,
            "w_in": f("w_in")[0],
            "conv_w": cwl,
            "dt_bias": f("dt_bias")[0].reshape(32, 1).copy(),
            "a_log": f("a_log")[0].reshape(32, 1).copy(),
            "pos": np.ascontiguousarray(np.broadcast_to(posv[None, :], (128, 1280))),
            "w_out": f("w_out")[0], "w_up": f("w_up")[0], "w_down": f("w_down")[0],
            "fcw": fcwl, "fcb": fcbl,
            "w_pg": f("w_ple_gate")[0], "w_pe": f("w_ple_proj")[0],
            "pT": np.ascontiguousarray(pfull[own0:own0 + 1024].T),
        }
        maps.append(m)
    return maps


def kernel(**inputs):
    nc = build_nc("full")
    maps = make_in_maps(inputs)
    res = run_bass_kernel_spmd(nc, maps, core_ids=list(range(NCORES)))
    outs = [r["out"] for r in res.results]
    full = np.concatenate([o.T for o in outs], axis=0)[None]
    return np.ascontiguousarray(full.astype(np.float32))
```
